# Optimizing a Trainium2 kernel written in Bass

```python
import math
import jax, jax.numpy as jnp
from jax import lax
import numpy as np

D_MODEL = 2048
BATCH = 2
SEQ = 4096
DEPTH = 1
DEC_BATCH = 32
DEC_SEQ = 16
PAST_LEN = 1024

CHUNK = 64
N_META = 16
D_SSM = D_MODEL // 2
SSM_GROUP = 16
N_SSM_GROUPS = D_SSM // SSM_GROUP
SSM_STATE = 64
D_POOL = D_MODEL - D_SSM
POOL_WINDOWS = (2, 4, 8, 16)
N_POOL_GROUPS = len(POOL_WINDOWS)
POOL_GROUP = D_POOL // N_POOL_GROUPS
POOL_HIST = max(POOL_WINDOWS) - 1
D_MIX = D_SSM + D_POOL
D_FF = 5632
FFN_CONV_W = 3
RMS_EPS = 1e-6
DT_MIN = 1e-3
DT_MAX = 1e-1

kernel_name = "hymba_s5_pool_convffn_stream_step"

F32 = jnp.float32


def rmsnorm(x, g):
    xf = x.astype(F32)
    y = xf * lax.rsqrt(jnp.mean(xf * xf, axis=-1, keepdims=True) + RMS_EPS)
    return (y * g.astype(F32)).astype(x.dtype)


def s5_mixer(u, h0_re, h0_im, lam_re, lam_im, log_dt, b_re, b_im, c_re, c_im, d_skip, w_glu):
    bsz, seqlen, _ = u.shape
    uf = u.astype(F32).reshape(bsz, seqlen, N_SSM_GROUPS, SSM_GROUP)
    dt = jnp.exp(log_dt.astype(F32))[:, None]
    lam = lax.complex(lam_re.astype(F32), lam_im.astype(F32))
    lam_bar = jnp.exp(lam * dt)
    b = lax.complex(b_re.astype(F32), b_im.astype(F32))
    b_bar = ((lam_bar - 1.0) / lam)[..., None] * b
    bu = jnp.einsum("gpn,blgn->blgp", b_bar, uf.astype(jnp.complex64))
    h0 = lax.complex(h0_re.astype(F32), h0_im.astype(F32))
    bu = bu.at[:, 0].add(lam_bar[None] * h0)
    a = jnp.broadcast_to(lam_bar, bu.shape)

    def combine(e1, e2):
        a1, b1 = e1
        a2, b2 = e2
        return a1 * a2, a2 * b1 + b2

    _, h = lax.associative_scan(combine, (a, bu), axis=1)
    c = lax.complex(c_re.astype(F32), c_im.astype(F32))
    y = jnp.einsum("gnp,blgp->blgn", c, h).real
    y = y + d_skip.astype(F32).reshape(N_SSM_GROUPS, SSM_GROUP) * uf
    y = jax.nn.gelu(y.reshape(bsz, seqlen, D_SSM))
    y = y * jax.nn.sigmoid(y @ w_glu.astype(F32))
    h_last = h[:, -1]
    return y.astype(u.dtype), h_last.real.astype(h0_re.dtype), h_last.imag.astype(h0_im.dtype)


def pool_mixer(u, hist, pos0, w_pool, pool_scale):
    bsz, seqlen, _ = u.shape
    z = jnp.concatenate([hist.astype(u.dtype), u], axis=1).astype(F32)
    cs = jnp.cumsum(z, axis=1)
    cs = jnp.concatenate([jnp.zeros_like(cs[:, :1]), cs], axis=1)
    end = cs[:, POOL_HIST + 1:]
    pos = pos0 + jnp.arange(seqlen)
    outs = []
    for k, w in enumerate(POOL_WINDOWS):
        lo, hi = k * POOL_GROUP, (k + 1) * POOL_GROUP
        start = cs[:, POOL_HIST + 1 - w: POOL_HIST + 1 - w + seqlen, lo:hi]
        cnt = jnp.minimum(pos + 1, w).astype(F32)[None, :, None]
        mean = (end[..., lo:hi] - start) / cnt
        outs.append(mean - z[:, POOL_HIST:, lo:hi])
    m = jnp.stack(outs, axis=2)
    m = jnp.einsum("blgc,gcd->blgd", m, w_pool.astype(F32)).reshape(bsz, seqlen, D_POOL)
    y = m * pool_scale.astype(F32)
    new_hist = z[:, -POOL_HIST:]
    return y.astype(u.dtype), new_hist.astype(hist.dtype)


def conv_ffn(x, hist, w_up, conv_w, conv_b, w_down):
    seqlen = x.shape[1]
    up = x @ w_up
    g, v = up[..., :D_FF], up[..., D_FF:]
    gext = jnp.concatenate([hist.astype(g.dtype), g], axis=1)
    gc = conv_b
    for k in range(FFN_CONV_W):
        gc = gc + gext[:, k:k + seqlen] * conv_w[k]
    h = jax.nn.gelu(gc) * v
    return h @ w_down, gext[:, -(FFN_CONV_W - 1):].astype(hist.dtype)


def layer(x, h_re, h_im, pool_hist, conv_hist, pos0,
          g_mix, w_in, lam_re, lam_im, log_dt, b_re, b_im, c_re, c_im, d_skip, w_glu,
          w_pool, pool_scale, w_out, g_ffn, w_up, conv_w, conv_b, w_down):
    hn = rmsnorm(x, g_mix)
    proj = hn @ w_in
    u_ssm, u_pool = proj[..., :D_SSM], proj[..., D_SSM:]
    y_ssm, new_re, new_im = s5_mixer(u_ssm, h_re, h_im, lam_re, lam_im, log_dt,
                                     b_re, b_im, c_re, c_im, d_skip, w_glu)
    y_pool, new_pool = pool_mixer(u_pool, pool_hist, pos0, w_pool, pool_scale)
    x = x + jnp.concatenate([y_ssm, y_pool], axis=-1) @ w_out
    f, new_conv = conv_ffn(rmsnorm(x, g_ffn), conv_hist, w_up, conv_w, conv_b, w_down)
    x = x + f
    return x, new_re, new_im, new_pool, new_conv


def trunk(x, h_re, h_im, pool_hist, conv_hist, pos0, layer_params):
    res_re, res_im, res_pool, res_conv = [], [], [], []
    for l in range(DEPTH):
        p = tuple(a[l] for a in layer_params)
        x, nr, ni, npool, nconv = layer(x, h_re[l], h_im[l], pool_hist[l], conv_hist[l], pos0, *p)
        res_re.append(nr)
        res_im.append(ni)
        res_pool.append(npool)
        res_conv.append(nconv)
    return x, jnp.stack(res_re), jnp.stack(res_im), jnp.stack(res_pool), jnp.stack(res_conv)


def setup_inputs(seed: int = 0) -> dict:
    key = jax.random.key(seed)
    ks = jax.random.split(key, 32)
    nrm = jax.random.normal
    lam_im0 = jnp.pi * jnp.arange(SSM_STATE, dtype=F32)
    return {
        "x_prompt": nrm(ks[0], (BATCH, SEQ, D_MODEL), F32),
        "x_sample": nrm(ks[1], (DEC_BATCH, DEC_SEQ, D_MODEL), F32),
        "state_ssm_re": nrm(ks[2], (DEPTH, DEC_BATCH, N_SSM_GROUPS, SSM_STATE), F32),
        "state_ssm_im": nrm(ks[3], (DEPTH, DEC_BATCH, N_SSM_GROUPS, SSM_STATE), F32),
        "state_pool": nrm(ks[4], (DEPTH, DEC_BATCH, POOL_HIST, D_POOL), F32),
        "state_ffn_conv": nrm(ks[5], (DEPTH, DEC_BATCH, FFN_CONV_W - 1, D_FF), F32),
        "meta_tokens": nrm(ks[6], (N_META, D_MODEL), F32),
        "g_mix": 1.0 + 0.02 * nrm(ks[7], (DEPTH, D_MODEL), F32),
        "w_in": nrm(ks[8], (DEPTH, D_MODEL, D_MIX), F32) * D_MODEL ** -0.5,
        "lam_re": -0.5 + 0.01 * nrm(ks[9], (DEPTH, N_SSM_GROUPS, SSM_STATE), F32),
        "lam_im": lam_im0 + 0.01 * nrm(ks[10], (DEPTH, N_SSM_GROUPS, SSM_STATE), F32),
        "log_dt": jax.random.uniform(ks[11], (DEPTH, N_SSM_GROUPS), F32,
                                     math.log(DT_MIN), math.log(DT_MAX)),
        "b_re": nrm(ks[12], (DEPTH, N_SSM_GROUPS, SSM_STATE, SSM_GROUP), F32) * (2 * SSM_GROUP) ** -0.5,
        "b_im": nrm(ks[13], (DEPTH, N_SSM_GROUPS, SSM_STATE, SSM_GROUP), F32) * (2 * SSM_GROUP) ** -0.5,
        "c_re": nrm(ks[14], (DEPTH, N_SSM_GROUPS, SSM_GROUP, SSM_STATE), F32) * (2 * SSM_STATE) ** -0.5,
        "c_im": nrm(ks[15], (DEPTH, N_SSM_GROUPS, SSM_GROUP, SSM_STATE), F32) * (2 * SSM_STATE) ** -0.5,
        "d_skip": 1.0 + 0.1 * nrm(ks[16], (DEPTH, D_SSM), F32),
        "w_glu": nrm(ks[17], (DEPTH, D_SSM, D_SSM), F32) * D_SSM ** -0.5,
        "w_pool": nrm(ks[18], (DEPTH, N_POOL_GROUPS, POOL_GROUP, POOL_GROUP), F32) * POOL_GROUP ** -0.5,
        "pool_scale": 1.0 + 0.1 * nrm(ks[19], (DEPTH, D_POOL), F32),
        "w_out": nrm(ks[20], (DEPTH, D_MIX, D_MODEL), F32) * D_MIX ** -0.5,
        "g_ffn": 1.0 + 0.02 * nrm(ks[21], (DEPTH, D_MODEL), F32),
        "w_up": nrm(ks[22], (DEPTH, D_MODEL, 2 * D_FF), F32) * D_MODEL ** -0.5,
        "conv_w": nrm(ks[23], (DEPTH, FFN_CONV_W, D_FF), F32) * FFN_CONV_W ** -0.5,
        "conv_b": 0.02 * nrm(ks[24], (DEPTH, D_FF), F32),
        "w_down": nrm(ks[25], (DEPTH, D_FF, D_MODEL), F32) * D_FF ** -0.5,
        "g_final": 1.0 + 0.02 * nrm(ks[26], (D_MODEL,), F32),
    }


def reference(x_prompt, x_sample, state_ssm_re, state_ssm_im, state_pool, state_ffn_conv,
              meta_tokens, g_mix, w_in, lam_re, lam_im, log_dt, b_re, b_im, c_re, c_im,
              d_skip, w_glu, w_pool, pool_scale, w_out, g_ffn, w_up, conv_w, conv_b,
              w_down, g_final):
    layer_params = (g_mix, w_in, lam_re, lam_im, log_dt, b_re, b_im, c_re, c_im, d_skip,
                    w_glu, w_pool, pool_scale, w_out, g_ffn, w_up, conv_w, conv_b, w_down)

    meta = jnp.broadcast_to(meta_tokens.astype(x_prompt.dtype)[None], (BATCH, N_META, D_MODEL))
    xp = jnp.concatenate([meta, x_prompt], axis=1)
    z_re = jnp.zeros((DEPTH, BATCH, N_SSM_GROUPS, SSM_STATE), state_ssm_re.dtype)
    z_im = jnp.zeros((DEPTH, BATCH, N_SSM_GROUPS, SSM_STATE), state_ssm_im.dtype)
    z_pool = jnp.zeros((DEPTH, BATCH, POOL_HIST, D_POOL), state_pool.dtype)
    z_conv = jnp.zeros((DEPTH, BATCH, FFN_CONV_W - 1, D_FF), state_ffn_conv.dtype)
    hp, ssm_re_p, ssm_im_p, pool_p, conv_p = trunk(xp, z_re, z_im, z_pool, z_conv, 0, layer_params)
    y_prompt = rmsnorm(hp, g_final)[:, N_META:]

    hs, ssm_re_s, ssm_im_s, pool_s, conv_s = trunk(x_sample, state_ssm_re, state_ssm_im,
                                                   state_pool, state_ffn_conv, PAST_LEN,
                                                   layer_params)
    y_sample = rmsnorm(hs, g_final)

    return (y_prompt, y_sample, ssm_re_p, ssm_im_p, pool_p, conv_p,
            ssm_re_s, ssm_im_s, pool_s, conv_s)
```

```python
import numpy as np
from contextlib import ExitStack
import concourse.bass as bass
import concourse.mybir as mybir
from concourse.bass_utils import run_bass_kernel_spmd

F32 = mybir.dt.float32
BF16 = mybir.dt.bfloat16
I32 = mybir.dt.int32
AF = mybir.ActivationFunctionType
ALU = mybir.AluOpType

class Prog:
    def __init__(self, nc, es, ndma=48):
        self.nc = nc
        self.names = ['pe', 'act', 'dve', 'pool', 'sp']
        self.ops = {n: [] for n in self.names}
        self.sem = {n: es.enter_context(nc.semaphore('s_' + n)) for n in self.names}
        self.cnt = {n: 0 for n in self.names}
        self.ndma = ndma
        self.dsem = [es.enter_context(nc.semaphore('d%d' % i)) for i in range(ndma)]
        self.dcnt = [0] * ndma
        self.dnext = 0
        self.dnext_sw = ndma // 2
        self.waited = {n: {} for n in self.names}
        self.lastw = {}
        self.lastw_extra = {}
        self.readers = {}
        self.out_toks = []

    def _need(self, eng, tok):
        if tok is None:
            return
        kind, sid, val = tok
        if kind == 'e' and sid == eng:
            if eng in ('pe', 'sp') or self.cnt[eng] - val >= 6:
                return
        key = (kind, sid)
        if self.waited[eng].get(key, 0) >= val:
            return
        self.waited[eng][key] = val
        sem = self.sem[sid] if kind == 'e' else self.dsem[sid]
        self.ops[eng].append(lambda e, sem=sem, val=val: e.wait_ge(sem, val))

    def _deps(self, eng, reads, writes):
        for k in reads:
            self._need(eng, self.lastw.get(k))
            for t in self.lastw_extra.get(k, ()):
                self._need(eng, t)
        for k in writes:
            self._need(eng, self.lastw.get(k))
            for t in self.lastw_extra.get(k, ()):
                self._need(eng, t)
            for t in self.readers.get(k, {}).values():
                self._need(eng, t)

    def _commit(self, tok, reads, writes):
        for k in reads:
            d = self.readers.setdefault(k, {})
            key = (tok[0], tok[1])
            if key not in d or d[key][2] < tok[2]:
                d[key] = tok
        for k in writes:
            prev = self.lastw.get(k)
            if tok[0] == 'd' and prev is not None and prev[0] == 'd' and not self.readers.get(k):
                self.lastw_extra.setdefault(k, []).append(prev)
            else:
                self.lastw_extra[k] = []
            self.lastw[k] = tok
            self.readers[k] = {}

    def op(self, eng, fn, reads=(), writes=()):
        self._deps(eng, reads, writes)
        self.cnt[eng] += 1
        sem = self.sem[eng]
        self.ops[eng].append(lambda e, fn=fn, sem=sem: fn(e).then_inc(sem, 1))
        self._commit(('e', eng, self.cnt[eng]), reads, writes)

    def dma(self, q, fn, reads=(), writes=(), is_out=False):
        self._deps(q, reads, writes)
        half = self.ndma // 2
        if q == 'pool':
            i = self.dnext_sw
            self.dnext_sw = half + (i + 1 - half) % half
        else:
            i = self.dnext
            self.dnext = (i + 1) % half
        if self.dcnt[i] > 0:
            self._need(q, ('d', i, self.dcnt[i]))
        self.dcnt[i] += 16
        sem = self.dsem[i]
        self.ops[q].append(lambda e, fn=fn, sem=sem: fn(e).then_inc(sem, 16))
        tok = ('d', i, self.dcnt[i])
        self._commit(tok, reads, writes)
        if is_out:
            self.out_toks.append(tok)

    def finish(self):
        for tok in self.out_toks:
            self._need('sp', tok)
        for n in ['pe', 'act', 'dve', 'pool']:
            if self.cnt[n] > 0:
                self._need('sp', ('e', n, self.cnt[n]))
        nc = self.nc
        with nc.Block() as block:
            @block.tensor
            def _(e):
                for f in self.ops['pe']:
                    f(e)

            @block.scalar
            def _(e):
                for f in self.ops['act']:
                    f(e)

            @block.vector
            def _(e):
                for f in self.ops['dve']:
                    f(e)

            @block.gpsimd
            def _(e):
                for f in self.ops['pool']:
                    f(e)

            @block.sync
            def _(e):
                for f in self.ops['sp']:
                    f(e)

    def barrier(self):
        snap = dict(self.cnt)
        dsnap = list(self.dcnt)
        for x in self.names:
            for y in self.names:
                if y != x and snap[y] > 0:
                    self._need(x, ('e', y, snap[y]))
            if x not in ('pe', 'sp') and snap[x] > 0 and self.waited[x].get(('e', x), 0) < snap[x]:
                self.waited[x][('e', x)] = snap[x]
                sem, val = self.sem[x], snap[x]
                self.ops[x].append(lambda e, sem=sem, val=val: e.wait_ge(sem, val))
            for i in range(self.ndma):
                if dsnap[i] > 0:
                    self._need(x, ('d', i, dsnap[i]))


D = 2048
E = 1180
WIN = 1056
OV = 28
SEG = 1028
NS = 4
PRE = 3072
NPT = 24
DFF = 5632
NJ = 44
EPS = 1e-6
TILES = [(i * 128, min(128, E - i * 128)) for i in range(10)]
BLKS = [(0, 512), (512, 512), (1024, 156)]
SCOL = [WIN + 31 * i for i in range(NS)]
ARENA = 49152
TWO_PI = float(2 * np.pi)


def build_nc():
    nc = bass.Bass("TRN2", target_bir_lowering=False)
    din = lambda name, shape: nc.dram_tensor(name, list(shape), F32, kind="ExternalInput").ap()
    dout = lambda name, shape: nc.dram_tensor(name, list(shape), F32, kind="ExternalOutput").ap()
    xall = din("xall", [PRE + E, D])
    lam_pg = din("lam_pg", [64, 3, 64])
    lam_hp = din("lam_hp", [128, 3, 32])
    lam_gm = din("lam_gm", [128, 3, 512])
    b_pg = din("b_pg", [64, 2, 1024])
    c_hp = din("c_hp", [128, 2, 1024])
    st_ssm = din("st_ssm", [128, 64, NS])
    st_pool = din("st_pool", [128, 8, NS, 15])
    st_conv = din("st_conv", [128, NJ, NS, 2])
    vecs_d = din("vecs", [128, 224])
    gfin_d = din("gfin", [128, D])
    cnt_d = din("cnt", [128, 4, E])
    ident_d = din("ident", [128, 128])
    masks_d = din("masks", [128, 72])
    w_in = din("w_in", [D, D])
    w_glu = din("w_glu", [1024, 1024])
    w_pool = din("w_pool", [4, 256, 256])
    w_out = din("w_out", [D, D])
    w_up = din("w_up", [D, 2 * DFF])
    w_down = din("w_down", [DFF, D])
    y_out = dout("y_out", [E, D])
    ssm_out = dout("ssm_out", [128, 64, 5])
    pool_out = dout("pool_out", [128, 8, 5, 15])
    conv_out = dout("conv_out", [128, NJ, 5, 2])

    with ExitStack() as es:
        P = Prog(nc, es)
        sbt = lambda name, shape, dt=F32: es.enter_context(nc.sbuf_tensor("sb_" + name, shape, dt))
        arena = sbt("arena", [128, ARENA], F32)
        ident = sbt("ident", [128, 128], F32)
        identb = sbt("identb", [128, 128], BF16)
        vecs = sbt("vecs", [128, 224], F32)
        masks = sbt("masks", [128, 72], F32)
        A1 = sbt("A1", [128, 64], F32)
        A2 = sbt("A2", [128, 64], F32)
        A1_, A2_ = A1, A2
        TMP = sbt("TMP", [128, 512], F32)
        TMP2 = sbt("TMP2", [128, 512], F32)
        A16_1 = sbt("A16_1", [128, 64], F32)
        A16_2 = sbt("A16_2", [128, 64], F32)
        HPRE = sbt("HPRE", [128, 64], F32)
        P16 = sbt("P16", [128, 5, 32], F32)
        SSMOUT = sbt("SSMOUT", [128, 64, 5], F32)
        POOLOUT = sbt("POOLOUT", [128, 8, 5, 15], F32)
        CONVOUT = sbt("CONVOUT", [128, NJ, 5, 2], F32)
        STS = sbt("STS", [128, 64, NS], F32)
        STC = sbt("STC", [128, NJ, NS, 2], F32)
        small = sbt("small", [128, 16], F32)
        epsb = sbt("epsb", [128, 1], F32)
        PS = es.enter_context(nc.psum_tensor("PS", [128, 4096], F32))
        banks = [PS[:, 512 * i:512 * (i + 1)] for i in range(8)]

        gmix = vecs[:, 0:16]
        gffn = vecs[:, 16:32]
        dsk = vecs[:, 32:40]
        psc = vecs[:, 40:48]
        cw = vecs[:, 48:180].rearrange("p (a b) -> p a b", a=3)
        cb = vecs[:, 180:224]
        maskB = masks[:, 0:8].rearrange("p (a b) -> p a b", a=4)
        maskC = masks[:, 8:40].rearrange("p (a b) -> p a b", a=4)
        maskCn = masks[:, 40:72].rearrange("p (a b) -> p a b", a=4)

        def V(off, shape, dt=F32, parts=128):
            n = int(np.prod(shape[1:]))
            w = n if dt != BF16 else (n + 1) // 2
            assert off + w <= ARENA, (off, w)
            ap = arena[0:parts, off:off + w]
            if dt != F32:
                ap = ap.bitcast(dt)
            if len(shape) == 3:
                ap = ap.rearrange("p (a b) -> p a b", a=shape[1])
            elif len(shape) == 4:
                ap = ap.rearrange("p (a b c) -> p a b c", a=shape[1], b=shape[2])
            return ap

        P.dma('sp', lambda e: e.dma_start(out=ident[:], in_=ident_d[:, :]), writes=['ident'])
        P.dma('sp', lambda e: e.dma_start(out=vecs[:], in_=vecs_d[:, :]), writes=['vecs'])
        P.dma('sp', lambda e: e.dma_start(out=masks[:], in_=masks_d[:, :]), writes=['masks'])
        P.dma('sp', lambda e: e.dma_start(out=STS[:], in_=st_ssm[:, :, :]), writes=['STS'])
        P.dma('sp', lambda e: e.dma_start(out=STC[:], in_=st_conv[:, :, :, :]), writes=['STC'])
        P.op('dve', lambda e: e.tensor_copy(identb[:], ident[:]), reads=['ident'], writes=['identb'])
        P.op('pool', lambda e: e.memset(epsb[:], EPS), writes=['epsb'])
        P.op('pool', lambda e: e.memset(SSMOUT[:], 0.0), writes=['SSMOUT'])

        BPAD = V(0, [128, 64, 128], BF16)
        CPAD = V(4096, [128, 64, 128], BF16)

        T0 = 16448

        def lambar(np_, ncol, src, base):
            t = {}
            names = ['raw0', 'raw1', 'raw2', 'dt', 'a', 'th', 'mag', 'cs', 'sn', 'lbr', 'lbi', 'w1', 'w2', 'w3']
            for i, nm in enumerate(names):
                t[nm] = V(base + i * ncol, [np_, ncol], F32, parts=np_)
            ti = V(base + len(names) * ncol, [np_, ncol], I32, parts=np_)
            raw = V(base, [np_, 3, ncol], F32, parts=np_)
            kp = 'lb%d_' % np_
            P.dma('sp', lambda e: e.dma_start(out=raw, in_=src[:, :, :]), writes=[kp + 'raw'])
            lr, li = t['raw0'], t['raw1']
            P.op('act', lambda e: e.activation(out=t['dt'], in_=t['raw2'], func=AF.Exp), reads=[kp + 'raw'], writes=[kp + 'dt'])
            P.op('dve', lambda e: e.tensor_tensor(out=t['a'], in0=lr, in1=t['dt'], op=ALU.mult), reads=[kp + 'raw', kp + 'dt'], writes=[kp + 'a'])
            P.op('dve', lambda e: e.tensor_tensor(out=t['th'], in0=li, in1=t['dt'], op=ALU.mult), reads=[kp + 'raw', kp + 'dt'], writes=[kp + 'th'])
            P.op('act', lambda e: e.activation(out=t['mag'], in_=t['a'], func=AF.Exp), reads=[kp + 'a'], writes=[kp + 'mag'])

            def sin_of(dst, shift):
                w1, w2, w3 = t['w1'], t['w2'], t['w3']
                k = kp + 'w'
                P.op('dve', lambda e: e.tensor_scalar(out=w1, in0=t['th'], scalar1=float(shift), scalar2=None, op0=ALU.add), reads=[kp + 'th'], writes=[k])
                P.op('dve', lambda e: e.tensor_scalar(out=w2, in0=w1, scalar1=1.0 / TWO_PI, scalar2=None, op0=ALU.mult), reads=[k], writes=[k])
                P.op('dve', lambda e: e.tensor_copy(ti, w2), reads=[k], writes=[k])
                P.op('dve', lambda e: e.tensor_copy(w2, ti), reads=[k], writes=[k])
                P.op('dve', lambda e: e.scalar_tensor_tensor(out=w1, in0=w2, scalar=-TWO_PI, in1=w1, op0=ALU.mult, op1=ALU.add), reads=[k], writes=[k])
                P.op('dve', lambda e: e.tensor_scalar(out=w3, in0=w1, scalar1=float(np.pi), scalar2=-TWO_PI, op0=ALU.is_gt, op1=ALU.mult), reads=[k], writes=[k])
                P.op('dve', lambda e: e.tensor_tensor(out=w1, in0=w1, in1=w3, op=ALU.add), reads=[k], writes=[k])
                P.op('dve', lambda e: e.tensor_scalar(out=w3, in0=w1, scalar1=float(-np.pi), scalar2=TWO_PI, op0=ALU.is_lt, op1=ALU.mult), reads=[k], writes=[k])
                P.op('dve', lambda e: e.tensor_tensor(out=w1, in0=w1, in1=w3, op=ALU.add), reads=[k], writes=[k])
                P.op('dve', lambda e: e.tensor_scalar(out=w1, in0=w1, scalar1=float(-np.pi), scalar2=float(np.pi), op0=ALU.max, op1=ALU.min), reads=[k], writes=[k])
                P.op('act', lambda e: e.activation(out=dst, in_=w1, func=AF.Sin), reads=[k], writes=[k, kp + 'sc'])

            sin_of(t['sn'], 0.0)
            sin_of(t['cs'], np.pi / 2)
            P.op('dve', lambda e: e.tensor_tensor(out=t['lbr'], in0=t['mag'], in1=t['cs'], op=ALU.mult), reads=[kp + 'mag', kp + 'sc'], writes=[kp + 'lb'])
            P.op('dve', lambda e: e.tensor_tensor(out=t['lbi'], in0=t['mag'], in1=t['sn'], op=ALU.mult), reads=[kp + 'mag', kp + 'sc'], writes=[kp + 'lb'])
            return t, kp

        th_, kp = lambar(128, 32, lam_hp, T0)
        A1v = A1[:, :].rearrange("p (q c j) -> p q c j", q=4, c=2)
        A2v_ = A2[:, :].rearrange("p (q c j) -> p q c j", q=4, c=2)
        A161v = A16_1[:, :].rearrange("p (q c j) -> p q c j", q=4, c=2)
        A162v = A16_2[:, :].rearrange("p (q c j) -> p q c j", q=4, c=2)
        q4 = lambda ap: ap.rearrange("p (q j) -> p q j", q=4)
        for c in range(2):
            P.op('dve', lambda e, c=c: e.tensor_copy(A1v[:, :, c, :], q4(th_['lbr'])), reads=[kp + 'lb'], writes=['A'])
        P.op('dve', lambda e: e.tensor_copy(A2v_[:, :, 1, :], q4(th_['lbi'])), reads=[kp + 'lb'], writes=['A'])
        P.op('dve', lambda e: e.tensor_scalar(out=A2v_[:, :, 0, :], in0=q4(th_['lbi']), scalar1=-1.0, scalar2=None, op0=ALU.mult), reads=[kp + 'lb'], writes=['A'])
        pr, pi_, s1, s2, s3 = (P16[:, i, :] for i in range(5))
        P.op('dve', lambda e: e.tensor_copy(pr, th_['lbr']), reads=[kp + 'lb'], writes=['P16'])
        P.op('dve', lambda e: e.tensor_copy(pi_, th_['lbi']), reads=[kp + 'lb'], writes=['P16'])
        A1h = TMP[:, 256:320]
        A2h = TMP[:, 320:384]
        A1hv = A1h.rearrange("p (q c j) -> p q c j", q=4, c=2)
        A2hv = A2h.rearrange("p (q c j) -> p q c j", q=4, c=2)
        for it_ in range(4):
            if it_ == 1:
                for c in range(2):
                    P.op('dve', lambda e, c=c: e.tensor_copy(A1hv[:, :, c, :], q4(pr)), reads=['P16'], writes=['Ah'])
                P.op('dve', lambda e: e.tensor_copy(A2hv[:, :, 1, :], q4(pi_)), reads=['P16'], writes=['Ah'])
                P.op('dve', lambda e: e.tensor_scalar(out=A2hv[:, :, 0, :], in0=q4(pi_), scalar1=-1.0, scalar2=None, op0=ALU.mult), reads=['P16'], writes=['Ah'])
            P.op('dve', lambda e: e.tensor_tensor(out=s1, in0=pr, in1=pr, op=ALU.mult), reads=['P16', 'Ah'] if it_ == 1 else ['P16'], writes=['P16'])
            P.op('dve', lambda e: e.tensor_tensor(out=s2, in0=pi_, in1=pi_, op=ALU.mult), reads=['P16'], writes=['P16'])
            P.op('dve', lambda e: e.tensor_tensor(out=s3, in0=pr, in1=pi_, op=ALU.mult), reads=['P16'], writes=['P16'])
            P.op('dve', lambda e: e.tensor_tensor(out=pr, in0=s1, in1=s2, op=ALU.subtract), reads=['P16'], writes=['P16'])
            P.op('dve', lambda e: e.tensor_scalar(out=pi_, in0=s3, scalar1=2.0, scalar2=None, op0=ALU.mult), reads=['P16'], writes=['P16'])
        for c in range(2):
            P.op('dve', lambda e, c=c: e.tensor_copy(A161v[:, :, c, :], q4(pr)), reads=['P16'], writes=['A16'])
        P.op('dve', lambda e: e.tensor_copy(A162v[:, :, 1, :], q4(pi_)), reads=['P16'], writes=['A16'])
        P.op('dve', lambda e: e.tensor_scalar(out=A162v[:, :, 0, :], in0=q4(pi_), scalar1=-1.0, scalar2=None, op0=ALU.mult), reads=['P16'], writes=['A16'])
        T1 = T0 + 15 * 32 + 32
        tp, kq = lambar(64, 64, lam_pg, T1)
        T2 = T1 + 16 * 64
        fr = V(T2, [64, 64], F32, parts=64)
        fi = V(T2 + 64, [64, 64], F32, parts=64)
        den = V(T2 + 128, [64, 64], F32, parts=64)
        nr = V(T2 + 192, [64, 64], F32, parts=64)
        q1 = V(T2 + 256, [64, 64], F32, parts=64)
        q2 = V(T2 + 320, [64, 64], F32, parts=64)
        lr, li, lbr, lbi = tp['raw0'], tp['raw1'], tp['lbr'], tp['lbi']
        kf = 'ftab'
        P.op('dve', lambda e: e.tensor_scalar(out=nr, in0=lbr, scalar1=-1.0, scalar2=None, op0=ALU.add), reads=[kq + 'lb'], writes=[kf])
        P.op('dve', lambda e: e.tensor_tensor(out=q1, in0=lr, in1=lr, op=ALU.mult), reads=[kq + 'raw'], writes=[kf])
        P.op('dve', lambda e: e.tensor_tensor(out=q2, in0=li, in1=li, op=ALU.mult), reads=[kq + 'raw'], writes=[kf])
        P.op('dve', lambda e: e.tensor_tensor(out=den, in0=q1, in1=q2, op=ALU.add), reads=[kf], writes=[kf])
        P.op('dve', lambda e: e.reciprocal(den, den), reads=[kf], writes=[kf])
        P.op('dve', lambda e: e.tensor_tensor(out=q1, in0=nr, in1=lr, op=ALU.mult), reads=[kf], writes=[kf])
        P.op('dve', lambda e: e.tensor_tensor(out=q2, in0=lbi, in1=li, op=ALU.mult), reads=[kf, kq + 'lb'], writes=[kf])
        P.op('dve', lambda e: e.tensor_tensor(out=q1, in0=q1, in1=q2, op=ALU.add), reads=[kf], writes=[kf])
        P.op('dve', lambda e: e.tensor_tensor(out=fr, in0=q1, in1=den, op=ALU.mult), reads=[kf], writes=[kf])
        P.op('dve', lambda e: e.tensor_tensor(out=q1, in0=lbi, in1=lr, op=ALU.mult), reads=[kf, kq + 'lb'], writes=[kf])
        P.op('dve', lambda e: e.tensor_tensor(out=q2, in0=nr, in1=li, op=ALU.mult), reads=[kf], writes=[kf])
        P.op('dve', lambda e: e.tensor_tensor(out=q1, in0=q1, in1=q2, op=ALU.subtract), reads=[kf], writes=[kf])
        P.op('dve', lambda e: e.tensor_tensor(out=fi, in0=q1, in1=den, op=ALU.mult), reads=[kf], writes=[kf])
        T3 = T2 + 384
        Braw = V(T3, [64, 2, 64, 16], F32, parts=64)
        Bbar = V(T3 + 2048, [64, 2, 1024], F32, parts=64)
        Bt1 = V(T3 + 4096, [64, 64, 16], F32, parts=64)
        Bt2 = V(T3 + 5120, [64, 64, 16], F32, parts=64)
        Bc = V(T3 + 6144, [128, 8, 2, 64], F32)
        P.dma('sp', lambda e: e.dma_start(out=V(T3, [64, 2, 1024], F32, parts=64), in_=b_pg[:, :, :]), writes=['Braw'])
        frb = fr.unsqueeze(2).to_broadcast([64, 64, 16])
        fib = fi.unsqueeze(2).to_broadcast([64, 64, 16])
        Bbar4 = V(T3 + 2048, [64, 2, 64, 16], F32, parts=64)
        P.op('pool', lambda e: e.tensor_tensor(out=Bt1, in0=Braw[:, 0], in1=frb, op=ALU.mult), reads=['Braw', kf], writes=['Bt1'])
        P.op('pool', lambda e: e.tensor_tensor(out=Bt2, in0=Braw[:, 1], in1=fib, op=ALU.mult), reads=['Braw', kf], writes=['Bt2'])
        P.op('pool', lambda e: e.tensor_tensor(out=Bbar4[:, 0], in0=Bt1, in1=Bt2, op=ALU.subtract), reads=['Bt1', 'Bt2'], writes=['Bbar'])
        P.op('pool', lambda e: e.tensor_tensor(out=Bt1, in0=Braw[:, 1], in1=frb, op=ALU.mult), reads=['Braw', kf, 'Bbar'], writes=['Bt1'])
        P.op('pool', lambda e: e.tensor_tensor(out=Bt2, in0=Braw[:, 0], in1=fib, op=ALU.mult), reads=['Braw', kf, 'Bbar'], writes=['Bt2'])
        P.op('pool', lambda e: e.tensor_tensor(out=Bbar4[:, 1], in0=Bt1, in1=Bt2, op=ALU.add), reads=['Bt1', 'Bt2'], writes=['Bbar'])
        for k in range(8):
            for c in range(2):
                bk = banks[(2 * k + c) % 2]
                key = 'bank%d' % ((2 * k + c) % 2)
                P.op('pe', lambda e, k=k, c=c, bk=bk: e.transpose(bk[:, 0:64], Bbar[:, c, k * 128:(k + 1) * 128], ident[0:64, 0:64]), reads=['Bbar', 'ident'], writes=[key])
                P.op('act', lambda e, k=k, c=c, bk=bk: e.activation(out=Bc[:, k, c, :], in_=bk[:, 0:64], func=AF.Copy), reads=[key], writes=['Bc'])
        for j in range(32):
            for c in range(2):
                P.op('pool', lambda e, j=j, c=c: e.tensor_tensor(
                    out=BPAD[:, c * 32 + j, :].rearrange("p (a b) -> p a b", a=2),
                    in0=Bc[:, j // 4, c, :].unsqueeze(1).to_broadcast([128, 2, 64]),
                    in1=maskB[:, j % 4, :].unsqueeze(2).to_broadcast([128, 2, 64]), op=ALU.mult),
                    reads=['Bc', 'masks'], writes=['BPAD'])
        tg, kg = lambar(128, 512, lam_gm, 28000)
        lgr = tg['lbr'].rearrange("p (k q) -> p k q", k=8)
        lgi = tg['lbi'].rearrange("p (k q) -> p k q", k=8)
        Bc1 = V(36000, [128, 8, 2, 64], F32)
        Bm1 = V(37024, [128, 8, 64], F32)
        Bm2 = V(37536, [128, 8, 64], F32)
        P.op('dve', lambda e: e.tensor_tensor(out=Bm1, in0=lgr, in1=Bc[:, :, 0, :], op=ALU.mult), reads=[kg + 'lb', 'Bc'], writes=['Bm1'])
        P.op('dve', lambda e: e.tensor_tensor(out=Bm2, in0=lgi, in1=Bc[:, :, 1, :], op=ALU.mult), reads=[kg + 'lb', 'Bc'], writes=['Bm2'])
        P.op('dve', lambda e: e.tensor_tensor(out=Bc1[:, :, 0, :], in0=Bm1, in1=Bm2, op=ALU.subtract), reads=['Bm1', 'Bm2'], writes=['Bc1'])
        P.op('dve', lambda e: e.tensor_tensor(out=Bm1, in0=lgr, in1=Bc[:, :, 1, :], op=ALU.mult), reads=[kg + 'lb', 'Bc', 'Bc1'], writes=['Bm1'])
        P.op('dve', lambda e: e.tensor_tensor(out=Bm2, in0=lgi, in1=Bc[:, :, 0, :], op=ALU.mult), reads=[kg + 'lb', 'Bc', 'Bc1'], writes=['Bm2'])
        P.op('dve', lambda e: e.tensor_tensor(out=Bc1[:, :, 1, :], in0=Bm1, in1=Bm2, op=ALU.add), reads=['Bm1', 'Bm2'], writes=['Bc1'])
        BPAD1 = V(4096, [128, 64, 128], BF16)
        for j in range(32):
            for c in range(2):
                P.op('pool', lambda e, j=j, c=c: e.tensor_tensor(
                    out=BPAD1[:, c * 32 + j, :].rearrange("p (a b) -> p a b", a=2),
                    in0=Bc1[:, j // 4, c, :].unsqueeze(1).to_broadcast([128, 2, 64]),
                    in1=maskB[:, j % 4, :].unsqueeze(2).to_broadcast([128, 2, 64]), op=ALU.mult),
                    reads=['Bc1', 'masks'], writes=['BPAD1'])
        P.barrier()

        NCH_H = 64
        NCH_W = WIN // 16
        VALL = V(8192, [128, 64, NCH_H], F32)
        TMPQ = [V(12288 + i * 1024, [128, 16, NCH_H], F32) for i in range(2)]
        VTS = [V(14336 + i * 4160, [128, 1 + NCH_H, 64], F32) for i in range(2)]
        UALLP = [V(22656 + i * 4096, [128, 8, 16 * NCH_H], BF16) for i in range(2)]
        WIN_SSM = V(30848, [128, 16, 1024], BF16)
        HNTP = V(39040, [128, 16, 512], BF16)
        XT = [V(43136 + i * 2048, [128, 2048], F32) for i in range(2)]
        XS = [V(47232, [128, 2048], BF16) for i in range(2)]
        SQJ = XS[0]
        WST = [V(40496 + i * 1024, [128, 16, 128], BF16) for i in range(2)]
        bank_bf = [b.bitcast(BF16) for b in banks]
        cnt_small = [0]

        def std_jc(jcp):
            q, r_ = jcp // 16, jcp % 16
            c, jl = r_ // 8, r_ % 8
            return c * 32 + 8 * q + jl, 8 * q + jl

        def rms_rstd(src, n, srckey, junk=None, junkkey='SQJ'):
            junk = SQJ if junk is None else junk
            i = cnt_small[0] % 16
            cnt_small[0] += 1
            s = small[0:n, i:i + 1]
            k = 'small%d' % i
            P.op('act', lambda e: e.activation(out=junk[0:n, :], in_=src, func=AF.Square, accum_out=s), reads=[srckey], writes=[k, junkkey])
            P.op('act', lambda e: e.activation(out=s, in_=s, func=AF.Sqrt, bias=epsb[0:n, :], scale=1.0 / D), reads=[k, 'epsb'], writes=[k])
            P.op('dve', lambda e: e.reciprocal(s, s), reads=[k], writes=[k])
            return s, k

        def norm_transpose(src, n, srckey, gvec, dst, dstkey, c0, slot, xsbuf=None, extra_w=()):
            xs = XS[slot] if xsbuf is None else xsbuf
            xk = 'XS' if xsbuf is None else 'XSF'
            s, k = rms_rstd(src, n, srckey, junk=xs, junkkey=xk)
            P.op('act', lambda e: e.activation(out=xs[0:n, :], in_=src, func=AF.Identity, scale=s), reads=[srckey, k], writes=[xk])
            for kk in range(16):
                b = kk // 8
                P.op('pe', lambda e, kk=kk, b=b: e.transpose(bank_bf[b][:, (kk % 8) * 128:(kk % 8) * 128 + n], xs[0:n, kk * 128:(kk + 1) * 128], identb[0:n, 0:n]),
                     reads=[xk, 'identb'], writes=['bank%d' % b])
            for b in range(2):
                P.op('dve', lambda e, b=b: e.tensor_tensor(out=dst[:, 8 * b:8 * b + 8, c0:c0 + n], in0=bank_bf[b].rearrange("p (k t) -> p k t", k=8)[:, :, 0:n],
                                                          in1=gvec[:, 8 * b:8 * b + 8].unsqueeze(2).to_broadcast([128, 8, n]), op=ALU.mult),
                     reads=['bank%d' % b, 'vecs'], writes=[dstkey] + list(extra_w))

        PSQ = PS[:, 2048:4096].rearrange("p (a b) -> p a b", a=16)

        def bu_quarter(q, rhs_of, n, ukey, rhs_of1=None):
            for i in range(16):
                jc, j = std_jc(16 * q + i)
                bk = 4 + i // 4
                rhs = rhs_of(j // 4)
                if rhs_of1 is not None:
                    rhs1 = rhs_of1(j // 4)
                    P.op('pe', lambda e, i=i, jc=jc, rhs1=rhs1: e.matmul(PSQ[:, i, 0:n], BPAD1[:, jc, :], rhs1, start=True, stop=False),
                         reads=['BPAD1', ukey], writes=['bank%d' % bk])
                P.op('pe', lambda e, i=i, jc=jc, rhs=rhs: e.matmul(PSQ[:, i, 0:n], BPAD[:, jc, :], rhs, start=(rhs_of1 is None), stop=True),
                     reads=['BPAD', ukey], writes=['bank%d' % bk])

        PSQK = ['bank4', 'bank5', 'bank6', 'bank7']

        def run(*gens):
            gens = list(gens)
            while gens:
                for g in list(gens):
                    try:
                        next(g)
                    except StopIteration:
                        gens.remove(g)

        def horner_gen(rhs_fn, ukey, nch, vall, tmpqs, vkp='VALL', two_step=False):
            A1, A2 = (A1h, A2h) if two_step else (A1_, A2_)
            akey = 'Ah' if two_step else 'A'
            for s in range(8 if two_step else 16):
                for q in range(4):
                    tmpq = tmpqs[q % 2]
                    tk = 'TMPQ%d' % (q % 2)
                    vq = vall[:, 16 * q:16 * q + 16, 0:nch]
                    vre = vall[:, 16 * q:16 * q + 8, 0:nch]
                    vim = vall[:, 16 * q + 8:16 * q + 16, 0:nch]
                    a1 = A1[:, 16 * q:16 * q + 16].unsqueeze(2).to_broadcast([128, 16, nch])
                    a2re = A2[:, 16 * q:16 * q + 8].unsqueeze(2).to_broadcast([128, 8, nch])
                    a2im = A2[:, 16 * q + 8:16 * q + 16].unsqueeze(2).to_broadcast([128, 8, nch])
                    vk = vkp + '%d' % q
                    if s > 0:
                        P.op('pool', lambda e, a2re=a2re, vim=vim, tmpq=tmpq: e.tensor_tensor(out=tmpq[:, 0:8, 0:nch], in0=a2re, in1=vim, op=ALU.mult), reads=[vk, akey], writes=[tk + 'a'])
                        P.op('pool', lambda e, a2im=a2im, vre=vre, tmpq=tmpq: e.tensor_tensor(out=tmpq[:, 8:16, 0:nch], in0=a2im, in1=vre, op=ALU.mult), reads=[vk, akey], writes=[tk + 'b'])
                    if two_step:
                        bu_quarter(q, lambda k, s=s: rhs_fn(k, 2 * s + 1), nch, ukey, rhs_of1=lambda k, s=s: rhs_fn(k, 2 * s))
                    else:
                        bu_quarter(q, lambda k, s=s: rhs_fn(k, s), nch, ukey)
                    if s == 0:
                        P.op('act', lambda e, vq=vq: e.activation(out=vq, in_=PSQ[:, :, 0:nch], func=AF.Copy), reads=PSQK, writes=[vk])
                    else:
                        P.op('dve', lambda e, vq=vq, a1=a1: e.tensor_tensor(out=vq, in0=vq, in1=a1, op=ALU.mult), reads=[akey, tk + 'a', tk + 'b'], writes=[vk])
                        P.op('dve', lambda e, vq=vq, tmpq=tmpq: e.tensor_tensor(out=vq, in0=vq, in1=tmpq[:, :, 0:nch], op=ALU.add), reads=[tk + 'a', tk + 'b'], writes=[vk])
                        P.op('dve', lambda e, vq=vq: e.tensor_tensor(out=vq, in0=vq, in1=PSQ[:, :, 0:nch], op=ALU.add), reads=PSQK, writes=[vk])
                    yield

        TMPs = TMP[:, 448:512]
        TMP2f = TMP2[:, 448:512]
        TMP2s = TMP2f.rearrange("p (q c j) -> p q c j", q=4, c=2)
        A16_2v = A16_2[:, :].rearrange("p (q c j) -> p q c j", q=4, c=2)

        def chunk_scan_gen(vt, nch, vtk, e2='pool'):
            for c in range(nch):
                prev = vt[:, c, :]
                prevv = prev.rearrange("p (q c j) -> p q c j", q=4, c=2)
                cur = vt[:, c + 1, :]
                P.op(e2, lambda e, prevv=prevv: e.tensor_tensor(out=TMP2s[:, :, 0, :], in0=A16_2v[:, :, 0, :], in1=prevv[:, :, 1, :], op=ALU.mult), reads=[vtk, 'A16'], writes=['TMP2ca'])
                P.op(e2, lambda e, prevv=prevv: e.tensor_tensor(out=TMP2s[:, :, 1, :], in0=A16_2v[:, :, 1, :], in1=prevv[:, :, 0, :], op=ALU.mult), reads=[vtk, 'A16'], writes=['TMP2cb'])
                P.op('dve', lambda e, prev=prev: e.tensor_tensor(out=TMPs, in0=A16_1[:, :], in1=prev, op=ALU.mult), reads=[vtk, 'A16'], writes=['TMPc'])
                P.op('dve', lambda e, cur=cur: e.tensor_tensor(out=cur, in0=cur, in1=TMPs, op=ALU.add), reads=['TMPc'], writes=[vtk])
                P.op('dve', lambda e, cur=cur: e.tensor_tensor(out=cur, in0=cur, in1=TMP2f, op=ALU.add), reads=['TMP2ca', 'TMP2cb'], writes=[vtk])
                yield

        for m in range(8):
            P.dma('pool', lambda e, m=m: e.dma_start(out=WIN_SSM[:, :, m * 128:(m + 1) * 128], in_=w_in[:, m * 128:(m + 1) * 128].rearrange("(k p) m -> p k m", p=128)), writes=['WIN_SSM'])
        NPART = PRE // (16 * NCH_H)

        def front_gen(p):
            ua = UALLP[p % 2]
            uk = 'UALLP%d' % (p % 2)
            for tl in range(8):
                ti = p * 8 + tl
                slot = ti % 2
                P.dma('sp', lambda e, ti=ti, slot=slot: e.dma_start(out=XT[slot][:, :], in_=xall[ti * 128:(ti + 1) * 128, :]), writes=['XT%d' % slot])
                norm_transpose(XT[slot][:, :], 128, 'XT%d' % slot, gmix, HNTP, 'HNTP', (ti % 4) * 128, slot)
                yield
                if ti % 4 == 3:
                    blk = tl // 4
                    for m in range(8):
                        bk = 2 + m % 2
                        for kk in range(16):
                            P.op('pe', lambda e, m=m, kk=kk, bk=bk: e.matmul(banks[bk][:, :], WIN_SSM[:, kk, m * 128:(m + 1) * 128], HNTP[:, kk, :], start=(kk == 0), stop=(kk == 15)),
                                 reads=['WIN_SSM', 'HNTP'], writes=['bank%d' % bk])
                        P.op('act', lambda e, m=m, bk=bk, blk=blk, ua=ua: e.activation(out=ua[:, m, blk * 512:(blk + 1) * 512], in_=banks[bk][:, :], func=AF.Copy), reads=['bank%d' % bk], writes=[uk])
                        yield

        def horner_part_gen(p):
            uav = UALLP[p % 2].rearrange("p k (c s) -> p k c s", s=16)
            vt = VTS[p % 2]
            yield from horner_gen(lambda k, s: uav[:, k, :, s], 'UALLP%d' % (p % 2), NCH_H, VALL, TMPQ, two_step=True)
            P.op('pool', lambda e, vt=vt: e.tensor_copy(vt[:, 1:1 + NCH_H, :].rearrange("p c j -> p j c"), VALL[:, :, :]), reads=['VALL0', 'VALL1', 'VALL2', 'VALL3'], writes=['vt%d' % (p % 2)])
            yield

        def cscan_part_gen(p):
            vt = VTS[p % 2]
            vtk = 'vt%d' % (p % 2)
            if p == 0:
                P.op('dve', lambda e, vt=vt: e.memset(vt[:, 0, :], 0.0), writes=[vtk])
            else:
                pv = VTS[(p - 1) % 2]
                P.op('dve', lambda e, vt=vt, pv=pv: e.tensor_copy(vt[:, 0, :], pv[:, NCH_H, :]), reads=['vt%d' % ((p - 1) % 2)], writes=[vtk])
            yield
            yield from chunk_scan_gen(vt, NCH_H, vtk, e2=('dve' if p == NPART - 1 else 'pool'))

        for st in range(NPART + 2):
            gens = []
            if st < NPART:
                gens.append(front_gen(st))
            if 0 <= st - 1 < NPART:
                gens.append(horner_part_gen(st - 1))
            if 0 <= st - 2 < NPART:
                gens.append(cscan_part_gen(st - 2))
            run(*gens)
        P.op('dve', lambda e: e.tensor_copy(HPRE[:, :], VTS[(NPART - 1) % 2][:, NCH_H, :]), reads=['vt%d' % ((NPART - 1) % 2)], writes=['HPRE'])
        P.barrier()

        HBW = V(8192, [128, 17, 64, 8], F32)
        USB = V(16896, [128, 8, E], BF16)
        Z = V(21616, [128, 8, E], F32)
        HNT = V(31056, [128, 16, E], BF16)
        for tt, (r0, n) in enumerate(TILES):
            slot = tt % 2
            P.dma('sp', lambda e, r0=r0, n=n, slot=slot: e.dma_start(out=XT[slot][0:n, :], in_=xall[PRE + r0:PRE + r0 + n, :]), writes=['XT%d' % slot])
            norm_transpose(XT[slot][0:n, :], n, 'XT%d' % slot, gmix, HNT, 'HNT', r0, slot)
        def m2_gen(ms):
            for m in ms:
                ws = m % 2
                P.dma('pool', lambda e, m=m, ws=ws: e.dma_start(out=WST[ws][:, :, :], in_=w_in[:, m * 128:(m + 1) * 128].rearrange("(k p) m -> p k m", p=128)), writes=['WST%d' % ws])
                for bi, (c0, n) in enumerate(BLKS):
                    bk = 2 + (m * 3 + bi) % 2
                    for kk in range(16):
                        P.op('pe', lambda e, kk=kk, bk=bk, ws=ws, c0=c0, n=n: e.matmul(banks[bk][:, 0:n], WST[ws][:, kk, :], HNT[:, kk, c0:c0 + n], start=(kk == 0), stop=(kk == 15)),
                             reads=['WST%d' % ws, 'HNT'], writes=['bank%d' % bk])
                    if m < 8:
                        P.op('act', lambda e, m=m, bk=bk, c0=c0, n=n: e.activation(out=USB[:, m, c0:c0 + n], in_=banks[bk][:, 0:n], func=AF.Copy), reads=['bank%d' % bk], writes=['USB'])
                    else:
                        P.op('act', lambda e, m=m, bk=bk, c0=c0, n=n: e.activation(out=Z[:, m - 8, c0:c0 + n], in_=banks[bk][:, 0:n], func=AF.Copy), reads=['bank%d' % bk], writes=['Z'])
                    yield

        run(m2_gen(range(0, 8)))

        HCF = V(31056, [128, 64, 128], BF16)
        YPRE = V(35152, [128, 8, E], F32)
        VTW = V(44592, [128, 1 + NCH_W, 64], F32)
        VALLW = V(8192, [128, 64, NCH_W], F32)
        TMPQW = [V(12416 + i * 1056, [128, 16, NCH_W], F32) for i in range(2)]
        USBw = USB[:, :, 0:WIN].rearrange("p k (c s) -> p k c s", s=16)
        run(m2_gen(range(8, 16)), horner_gen(lambda k, s: USBw[:, k, :, s], 'USB', NCH_W, VALLW, TMPQW, vkp='VALW', two_step=True))
        P.barrier()
        P.op('dve', lambda e: e.tensor_copy(VTW[:, 0, :], HPRE[:, :]), reads=['HPRE'], writes=['vt'])
        P.op('pool', lambda e: e.tensor_copy(VTW[:, 1:1 + NCH_W, :].rearrange("p c j -> p j c"), VALLW[:, :, :]), reads=['VALW0', 'VALW1', 'VALW2', 'VALW3', 'vt'], writes=['vt'])
        Craw = V(14528, [128, 2, 64, 16], F32)
        P.dma('sp', lambda e: e.dma_start(out=V(14528, [128, 2, 1024], F32), in_=c_hp[:, :, :]), writes=['Craw'])
        for j in range(32):
            for c in range(2):
                mk = maskC if c == 0 else maskCn
                P.op('pool', lambda e, j=j, c=c, mk=mk: e.tensor_tensor(
                    out=CPAD[:, c * 32 + j, :].rearrange("p (a b) -> p a b", a=8),
                    in0=Craw[:, c, 8 * (j // 4):8 * (j // 4) + 8, :],
                    in1=mk[:, j % 4, :].unsqueeze(2).to_broadcast([128, 8, 16]), op=ALU.mult),
                    reads=['Craw', 'masks'], writes=['CPAD', 'BPAD1'])
        run(chunk_scan_gen(VTW, NCH_W, 'vt', e2='dve'))
        P.op('dve', lambda e: e.tensor_copy(SSMOUT[:, :, 4], VTW[:, NCH_W, :]), reads=['vt'], writes=['SSMOUT'])
        P.barrier()

        def y_block_gen(c0, n, HCAST=None, hck='HCAST'):
            HCAST = HCF if HCAST is None else HCAST
            for half in range(2):
                for k in range(4 * half, 4 * half + 4):
                    bk = 2 + k // 4
                    idx = 0
                    for j in range(4 * k, 4 * k + 4):
                        for c in range(2):
                            jcp = 16 * (j // 8) + 8 * c + (j % 8)
                            P.op('pe', lambda e, jcp=jcp, c=c, j=j, bk=bk, k=k, idx=idx: e.matmul(banks[bk][:, (k % 4) * 128:(k % 4) * 128 + n], CPAD[:, c * 32 + j, :], HCAST[:, jcp, 0:n], start=(idx == 0), stop=(idx == 7)),
                                 reads=['CPAD', hck], writes=['bank%d' % bk])
                            idx += 1
                    yield
                for k in range(4 * half, 4 * half + 4):
                    bk = 2 + k // 4
                    P.op('dve', lambda e, k=k, bk=bk: e.scalar_tensor_tensor(out=YPRE[:, k, c0:c0 + n], in0=USB[:, k, c0:c0 + n], scalar=dsk[:, k:k + 1], in1=banks[bk][:, (k % 4) * 128:(k % 4) * 128 + n], op0=ALU.mult, op1=ALU.add),
                         reads=['USB', 'vecs', 'bank%d' % bk], writes=['YPRE'])
                yield

        def y_block(c0, n, HCAST=None, hck='HCAST'):
            run(y_block_gen(c0, n, HCAST, hck))

        def scan_block_gen(hbw, nseq, hk, slot):
            a1 = A1[:, :].unsqueeze(2).to_broadcast([128, 64, nseq])
            a2v = A2[:, :].rearrange("p (q c j) -> p q c j", q=4, c=2)
            o = 256 * slot
            t2f = TMP2[:, o:o + 64 * nseq]
            t2 = t2f.rearrange("p (q c j i) -> p q c j i", q=4, c=2, j=8)
            t1 = TMP[:, o:o + 64 * nseq].rearrange("p (a i) -> p a i", i=nseq)
            ks = '_s%d' % slot
            for t in range(16):
                prev = hbw[:, t, :, :]
                cur = hbw[:, t + 1, :, :]
                prevv = prev.rearrange("p (q c j) i -> p q c j i", q=4, c=2)
                P.op('pool', lambda e, prevv=prevv: e.tensor_tensor(out=t2[:, :, 0, :, :], in0=a2v[:, :, 0, :].unsqueeze(3).to_broadcast([128, 4, 8, nseq]), in1=prevv[:, :, 1, :, :], op=ALU.mult), reads=[hk, 'A'], writes=['TMP2a' + ks])
                P.op('pool', lambda e, prevv=prevv: e.tensor_tensor(out=t2[:, :, 1, :, :], in0=a2v[:, :, 1, :].unsqueeze(3).to_broadcast([128, 4, 8, nseq]), in1=prevv[:, :, 0, :, :], op=ALU.mult), reads=[hk, 'A'], writes=['TMP2b' + ks])
                P.op('dve', lambda e, prev=prev: e.tensor_tensor(out=t1, in0=a1, in1=prev, op=ALU.mult), reads=[hk, 'A'], writes=['TMP' + ks])
                P.op('dve', lambda e, cur=cur: e.tensor_tensor(out=cur, in0=cur, in1=t1, op=ALU.add), reads=['TMP' + ks], writes=[hk])
                P.op('dve', lambda e, cur=cur: e.tensor_tensor(out=cur, in0=cur, in1=t2f.rearrange("p (a i) -> p a i", i=nseq), op=ALU.add), reads=['TMP2a' + ks, 'TMP2b' + ks], writes=[hk])
                yield

        def scan_multi_gen(blks):
            a2v = A2[:, :].rearrange("p (q c j) -> p q c j", q=4, c=2)
            info = []
            for (hbw, nseq, hk, slot) in blks:
                o = 256 * slot
                t2f = TMP2[:, o:o + 64 * nseq]
                info.append(dict(hbw=hbw, nseq=nseq, hk=hk, ks='_s%d' % slot,
                                 a1=A1[:, :].unsqueeze(2).to_broadcast([128, 64, nseq]),
                                 t2f=t2f, t2=t2f.rearrange("p (q c j i) -> p q c j i", q=4, c=2, j=8),
                                 t1=TMP[:, o:o + 64 * nseq].rearrange("p (a i) -> p a i", i=nseq)))
            for t in range(16):
                for d in info:
                    prevv = d['hbw'][:, t, :, :].rearrange("p (q c j) i -> p q c j i", q=4, c=2)
                    n_ = d['nseq']
                    P.op('pool', lambda e, d=d, prevv=prevv, n_=n_: e.tensor_tensor(out=d['t2'][:, :, 0, :, :], in0=a2v[:, :, 0, :].unsqueeze(3).to_broadcast([128, 4, 8, n_]), in1=prevv[:, :, 1, :, :], op=ALU.mult), reads=[d['hk'], 'A'], writes=['TMP2a' + d['ks']])
                    P.op('pool', lambda e, d=d, prevv=prevv, n_=n_: e.tensor_tensor(out=d['t2'][:, :, 1, :, :], in0=a2v[:, :, 1, :].unsqueeze(3).to_broadcast([128, 4, 8, n_]), in1=prevv[:, :, 0, :, :], op=ALU.mult), reads=[d['hk'], 'A'], writes=['TMP2b' + d['ks']])
                for d in info:
                    prev = d['hbw'][:, t, :, :]
                    P.op('dve', lambda e, d=d, prev=prev: e.tensor_tensor(out=d['t1'], in0=d['a1'], in1=prev, op=ALU.mult), reads=[d['hk'], 'A'], writes=['TMP' + d['ks']])
                for d in info:
                    cur = d['hbw'][:, t + 1, :, :]
                    P.op('dve', lambda e, d=d, cur=cur: e.tensor_tensor(out=cur, in0=cur, in1=d['t1'], op=ALU.add), reads=['TMP' + d['ks']], writes=[d['hk']])
                for d in info:
                    cur = d['hbw'][:, t + 1, :, :]
                    n_ = d['nseq']
                    P.op('dve', lambda e, d=d, cur=cur, n_=n_: e.tensor_tensor(out=cur, in0=cur, in1=d['t2f'].rearrange("p (a i) -> p a i", i=n_), op=ALU.add), reads=['TMP2a' + d['ks'], 'TMP2b' + d['ks']], writes=[d['hk']])
                yield

        BT = 32
        HBW4 = [V(8192 + i * 2176, [128, 17, 64, 2], F32) for i in range(4)]
        HC4 = [V(31056 + i * 1024, [128, 64, BT], BF16) for i in range(4)]
        nblk = WIN // BT
        pairs = [[b for b in (2 * k, 2 * k + 1) if b < nblk] for k in range((nblk + 1) // 2)]

        def prep_gen(pair):
            for b in pair:
                hbw = HBW4[b % 4]
                hk = 'hb%d' % (b % 4)
                c0 = b * BT
                P.op('pool', lambda e, b=b, hbw=hbw: e.tensor_copy(hbw[:, 0, :, :], VTW[:, 2 * b:2 * b + 2, :].rearrange("p c j -> p j c")), reads=['vt'], writes=[hk])
                for q in range(4):
                    bu_quarter(q, lambda k, c0=c0: USB[:, k, c0:c0 + BT], BT, 'USB')
                    P.op('act', lambda e, q=q, hbw=hbw: e.activation(
                        out=hbw[:, 1:17, 16 * q:16 * q + 16, :].rearrange("p t j c -> p j c t"),
                        in_=PSQ[:, :, 0:BT].rearrange("p j (c t) -> p j c t", t=16), func=AF.Copy),
                        reads=PSQK, writes=[hk])
                    yield

        def ypost_gen(pair):
            for b in pair:
                yield from y_block_gen(b * BT, BT, HC4[b % 4], 'HCAST%d' % (b % 4))

        run(prep_gen(pairs[0]))
        for k in range(len(pairs) + 1):
            gens = []
            if k < len(pairs):
                gens.append(scan_multi_gen([(HBW4[b % 4], 2, 'hb%d' % (b % 4), b % 2) for b in pairs[k]]))
            if k + 1 < len(pairs):
                gens.append(prep_gen(pairs[k + 1]))
            if k >= 1:
                gens.append(ypost_gen(pairs[k - 1]))
            run(*gens)
            if k < len(pairs):
                for b in pairs[k]:
                    hbw, hc = HBW4[b % 4], HC4[b % 4]
                    P.op('act', lambda e, hbw=hbw, hc=hc: e.activation(out=hc[:, :, :].rearrange("p j (c t) -> p j c t", t=16), in_=hbw[:, 1:17, :, :].rearrange("p t j c -> p j c t"), func=AF.Copy), reads=['hb%d' % (b % 4)], writes=['HCAST%d' % (b % 4)])
        P.barrier()
        ns = E - WIN
        HBS = V(8192, [128, 17, 64, 4], F32)
        P.op('pool', lambda e: e.tensor_copy(HBS[:, 0, :, :], STS[:, :, :]), reads=['STS'], writes=['hbS'])
        P.op('pool', lambda e: e.memset(HCF[:, :, 0:ns], 0.0), reads=[], writes=['HCAST'])
        for q in range(4):
            bu_quarter(q, lambda k: USB[:, k, WIN:E], ns, 'USB')
            for bb in range(4):
                P.op('act', lambda e, q=q, bb=bb: e.activation(
                    out=HBS[:, 1:17, 16 * q + 4 * bb:16 * q + 4 * bb + 4, :].rearrange("p t j c -> p j c t"),
                    in_=PSQ[:, 4 * bb:4 * bb + 4, 0:ns].rearrange("p j (c t) -> p j c t", c=NS)[:, :, :, 15:31], func=AF.Copy),
                    reads=['bank%d' % (4 + bb)], writes=['hbS'])
        run(scan_block_gen(HBS, NS, 'hbS', 0))
        P.op('act', lambda e: e.activation(out=HCF[:, :, 0:ns].rearrange("p j (c t) -> p j c t", c=NS)[:, :, :, 15:31], in_=HBS[:, 1:17, :, :].rearrange("p t j c -> p j c t"), func=AF.Copy), reads=['hbS'], writes=['HCAST'])
        P.op('dve', lambda e: e.tensor_copy(SSMOUT[:, :, 0:4], HBS[:, 16, :, :]), reads=['hbS'], writes=['SSMOUT'])
        y_block(WIN, ns)
        P.dma('sp', lambda e: e.dma_start(out=ssm_out[:, :, :], in_=SSMOUT[:]), reads=['SSMOUT'], is_out=True)
        P.barrier()

        YMIXB = V(0, [128, 16, E], BF16)
        YSB = V(9440, [128, 8, E], BF16)
        WGLU = V(14160, [128, 8, 1024], BF16)
        SG = [V(18256 + i * 512, [128, 512], F32) for i in range(2)]
        for k in range(8):
            P.dma('pool', lambda e, k=k: e.dma_start(out=WGLU[:, k, :], in_=w_glu[k * 128:(k + 1) * 128, :]), writes=['WGLU'])
        for k in range(8):
            P.op('act', lambda e, k=k: e.activation(out=YPRE[:, k, :], in_=YPRE[:, k, :], func=AF.Gelu_apprx_tanh), reads=['YPRE'], writes=['YPRE'])
            P.op('pool', lambda e, k=k: e.tensor_copy(YSB[:, k, :], YPRE[:, k, :]), reads=['YPRE'], writes=['YSB'])
        it = 0
        for m in range(8):
            for (c0, n) in BLKS:
                bk = 2 + it % 2
                sg = SG[it % 2]
                sk = 'SG%d' % (it % 2)
                it += 1
                for kk in range(8):
                    P.op('pe', lambda e, m=m, kk=kk, bk=bk, c0=c0, n=n: e.matmul(banks[bk][:, 0:n], WGLU[:, kk, m * 128:(m + 1) * 128], YSB[:, kk, c0:c0 + n], start=(kk == 0), stop=(kk == 7)),
                         reads=['WGLU', 'YSB'], writes=['bank%d' % bk])
                P.op('act', lambda e, bk=bk, sg=sg, n=n: e.activation(out=sg[:, 0:n], in_=banks[bk][:, 0:n], func=AF.Sigmoid), reads=['bank%d' % bk], writes=[sk])
                P.op('pool', lambda e, m=m, sg=sg, c0=c0, n=n: e.tensor_tensor(out=YMIXB[:, m, c0:c0 + n], in0=YPRE[:, m, c0:c0 + n], in1=sg[:, 0:n], op=ALU.mult), reads=['YPRE', sk], writes=['YMIXB'])
        P.barrier()

        PA = V(31056, [128, 2, E], F32)
        PB = V(33416, [128, 2, E], F32)
        MB = V(35776, [128, 8, E], BF16)
        RCN = V(40496, [128, 4, E], F32)
        WPOOL = V(45216, [128, 4, 2, 256], BF16)
        P.dma('sp', lambda e: e.dma_start(out=RCN, in_=cnt_d[:, :, :]), writes=['RCN'])
        P.dma('pool', lambda e: e.dma_start(out=WPOOL, in_=w_pool.rearrange("g (ki p) d -> p g ki d", p=128)), writes=['WPOOL'])
        Zs = Z[:, :, WIN:E].rearrange("p k (i c) -> p k i c", i=NS)
        for k in range(8):
            P.dma('sp', lambda e, k=k: e.dma_start(out=Zs[:, k, :, 0:15], in_=st_pool[:, k, :, :]), writes=['Z'])
        P.op('dve', lambda e: e.reciprocal(RCN, RCN), reads=['RCN'], writes=['RCN'])
        P.op('pool', lambda e: e.tensor_copy(POOLOUT[:, :, 4, :], Z[:, :, WIN - 15:WIN]), reads=['Z'], writes=['POOLOUT'])
        P.op('pool', lambda e: e.tensor_copy(POOLOUT[:, :, 0:4, :], Zs[:, :, :, 16:31]), reads=['Z'], writes=['POOLOUT'])
        P.dma('sp', lambda e: e.dma_start(out=pool_out[:, :, :, :], in_=POOLOUT[:]), reads=['POOLOUT'], is_out=True)
        PC = V(9440, [128, 2, E], F32)
        PD = V(11800, [128, 2, E], F32)
        for kk in [2, 0, 3, 1]:
            eng = 'dve' if kk >= 2 else 'pool'
            Zg = Z[:, 2 * kk:2 * kk + 2, :]
            cur, ck = Zg, 'Z'
            bufs = [(PC, 'PC'), (PD, 'PD')] if eng == 'dve' else [(PA, 'PA'), (PB, 'PB')]
            bi = 0
            for d in [1, 2, 4, 8][:kk + 1]:
                nxt, nk = bufs[bi % 2]
                bi += 1
                P.op(eng, lambda e, cur=cur, nxt=nxt, d=d: e.tensor_tensor(out=nxt[:, :, d:E], in0=cur[:, :, d:E], in1=cur[:, :, 0:E - d], op=ALU.add), reads=[ck], writes=[nk])
                P.op(eng, lambda e, cur=cur, nxt=nxt, d=d: e.tensor_copy(nxt[:, :, 0:d], cur[:, :, 0:d]), reads=[ck], writes=[nk])
                cur, ck = nxt, nk
            nxt, nk = bufs[bi % 2]
            P.op(eng, lambda e, cur=cur, nxt=nxt, kk=kk: e.tensor_tensor(out=nxt, in0=cur, in1=RCN[:, kk, :].unsqueeze(1).to_broadcast([128, 2, E]), op=ALU.mult), reads=[ck, 'RCN'], writes=[nk])
            P.op(eng, lambda e, nxt=nxt, kk=kk, Zg=Zg: e.tensor_tensor(out=MB[:, 2 * kk:2 * kk + 2, :], in0=nxt, in1=Zg, op=ALU.subtract), reads=[nk, 'Z'], writes=['MB%d' % kk])
            for mo in range(2):
                for (c0, n) in BLKS:
                    bk = 2 + it % 2
                    it += 1
                    for ki in range(2):
                        P.op('pe', lambda e, kk=kk, ki=ki, mo=mo, bk=bk, c0=c0, n=n: e.matmul(banks[bk][:, 0:n], WPOOL[:, kk, ki, mo * 128:(mo + 1) * 128], MB[:, 2 * kk + ki, c0:c0 + n], start=(ki == 0), stop=(ki == 1)),
                             reads=['WPOOL', 'MB%d' % kk], writes=['bank%d' % bk])
                    ch = 2 * kk + mo
                    P.op('act', lambda e, ch=ch, bk=bk, c0=c0, n=n: e.activation(out=YMIXB[:, 8 + ch, c0:c0 + n], in_=banks[bk][:, 0:n], func=AF.Identity, scale=psc[:, ch:ch + 1]), reads=['bank%d' % bk, 'vecs'], writes=['YMIXB'])
        P.barrier()

        WOUT = V(9440, [128, 16, 2048], BF16)
        X1 = V(28672, [128, 10, 2048], F32)
        HN2T = V(0, [128, 16, E], BF16)
        XSF = V(26456, [128, 2048], BF16)
        for k in range(16):
            for half in range(2):
                P.dma('pool', lambda e, k=k, half=half: e.dma_start(out=WOUT[:, k, half * 1024:(half + 1) * 1024], in_=w_out[k * 128:(k + 1) * 128, half * 1024:(half + 1) * 1024]), writes=['WOUT'])
        for tt, (r0, n) in enumerate(TILES):
            P.dma('sp', lambda e, tt=tt, r0=r0, n=n: e.dma_start(out=X1[0:n, tt, :], in_=xall[PRE + r0:PRE + r0 + n, :]), writes=['X1_%d' % tt])
        for tt, (r0, n) in enumerate(TILES):
            for nb in range(4):
                bk = 4 + it % 4
                it += 1
                for kk in range(16):
                    P.op('pe', lambda e, kk=kk, bk=bk, r0=r0, n=n, nb=nb: e.matmul(banks[bk][0:n, :], YMIXB[:, kk, r0:r0 + n], WOUT[:, kk, nb * 512:(nb + 1) * 512], start=(kk == 0), stop=(kk == 15)),
                         reads=['YMT%d' % tt, 'WOUT'], writes=['bank%d' % bk])
                P.op('dve', lambda e, tt=tt, bk=bk, n=n, nb=nb: e.tensor_tensor(out=X1[0:n, tt, nb * 512:(nb + 1) * 512], in0=X1[0:n, tt, nb * 512:(nb + 1) * 512], in1=banks[bk][0:n, :], op=ALU.add),
                     reads=['bank%d' % bk, 'X1_%d' % tt], writes=['X1_%d' % tt])
            norm_transpose(X1[0:n, tt, :], n, 'X1_%d' % tt, gffn, HN2T, 'HN2T', r0, 0, xsbuf=XSF, extra_w=['YMT%d' % tt])
        P.barrier()

        WUP = [[V(9440 + (s_ * 2 + gv) * 1024, [128, 16, 128], BF16) for gv in range(2)] for s_ in range(3)]
        WDN = [[V(15584 + (g_ * 2 + ci) * 1024, [128, 2048], BF16) for ci in range(2)] for g_ in range(2)]
        HG = [V(19680 + g_ * 1180, [128, 2, E], BF16) for g_ in range(2)]
        GBUF = [V(22040 + i * 1184, [128, E + 2], F32) for i in range(2)]
        GC = [V(24408 + i * 512, [128, 512], F32) for i in range(2)]
        for i in range(2):
            P.op('pool', lambda e, i=i: e.memset(GBUF[i][:, 0:2], 0.0), writes=['GBUF%d' % i])
        gcn = [0]

        def load_wup(j):
            s_ = j % 3
            for gv in range(2):
                P.dma('pool', lambda e, j=j, gv=gv, s_=s_: e.dma_start(out=WUP[s_][gv][:, :, :], in_=w_up[:, gv * DFF + j * 128:gv * DFF + (j + 1) * 128].rearrange("(k p) m -> p k m", p=128)), writes=['WUP%d_%d' % (s_, gv)])

        def load_wdn(jg):
            for ci in range(2):
                j = 2 * jg + ci
                for half in range(2):
                    P.dma('pool', lambda e, j=j, jg=jg, ci=ci, half=half: e.dma_start(out=WDN[jg % 2][ci][:, half * 1024:(half + 1) * 1024], in_=w_down[j * 128:(j + 1) * 128, half * 1024:(half + 1) * 1024]), writes=['WDN%d' % (jg % 2)])

        def up_unit(jg, ci, bi):
            hg = HG[jg % 2]
            hk = 'HG%d' % (jg % 2)
            j = 2 * jg + ci
            s_ = j % 3
            gb = GBUF[j % 2]
            gk = 'GBUF%d' % (j % 2)
            c0, n = BLKS[bi]
            pg = (j * 3 + bi) % 2
            pv = 2 + (j * 3 + bi) % 2
            for gv, bk in ((0, pg), (1, pv)):
                for kk in range(16):
                    P.op('pe', lambda e, gv=gv, bk=bk, kk=kk: e.matmul(banks[bk][:, 0:n], WUP[s_][gv][:, kk, :], HN2T[:, kk, c0:c0 + n], start=(kk == 0), stop=(kk == 15)),
                         reads=['WUP%d_%d' % (s_, gv), 'HN2T'], writes=['bank%d' % bk])
            P.op('act', lambda e: e.activation(out=gb[:, 2 + c0:2 + c0 + n], in_=banks[pg][:, 0:n], func=AF.Copy), reads=['bank%d' % pg], writes=[gk])
            if bi == 2:
                gs = gb[:, 2 + WIN:2 + E].rearrange("p (i c) -> p i c", i=NS)
                P.op('pool', lambda e: e.tensor_copy(CONVOUT[:, j, 4, :], gb[:, 2 + WIN - 2:2 + WIN]), reads=[gk], writes=['CONVOUT'])
                P.op('pool', lambda e: e.tensor_copy(CONVOUT[:, j, 0:4, :], gs[:, :, 29:31]), reads=[gk], writes=['CONVOUT'])
                P.op('pool', lambda e: e.tensor_copy(gs[:, :, 13:15], STC[:, j, :, :]), reads=['STC', gk], writes=[gk])
            gc = GC[gcn[0] % 2]
            gck = 'GC%d' % (gcn[0] % 2)
            gcn[0] += 1
            P.op('act', lambda e: e.activation(out=gc[:, 0:n], in_=gb[:, 2 + c0:2 + c0 + n], func=AF.Identity, scale=cw[:, 2, j:j + 1], bias=cb[:, j:j + 1]), reads=[gk, 'vecs'], writes=[gck])
            P.op('dve', lambda e: e.scalar_tensor_tensor(out=gc[:, 0:n], in0=gb[:, 1 + c0:1 + c0 + n], scalar=cw[:, 1, j:j + 1], in1=gc[:, 0:n], op0=ALU.mult, op1=ALU.add), reads=[gk, 'vecs', gck], writes=[gck])
            P.op('dve', lambda e: e.scalar_tensor_tensor(out=gc[:, 0:n], in0=gb[:, c0:c0 + n], scalar=cw[:, 0, j:j + 1], in1=gc[:, 0:n], op0=ALU.mult, op1=ALU.add), reads=[gk, 'vecs', gck], writes=[gck])
            P.op('act', lambda e: e.activation(out=gc[:, 0:n], in_=gc[:, 0:n], func=AF.Gelu_apprx_tanh), reads=[gck], writes=[gck])
            P.op('dve', lambda e: e.tensor_tensor(out=hg[:, ci, c0:c0 + n], in0=gc[:, 0:n], in1=banks[pv][:, 0:n], op=ALU.mult), reads=[gck, 'bank%d' % pv], writes=[hk])

        dcnt = [0]

        def down_unit(jg, tt, nb):
            hg = HG[jg % 2]
            hk = 'HG%d' % (jg % 2)
            r0, n = TILES[tt]
            bk = 4 + dcnt[0] % 4
            dcnt[0] += 1
            for ci in range(2):
                P.op('pe', lambda e, ci=ci: e.matmul(banks[bk][0:n, :], hg[:, ci, r0:r0 + n], WDN[jg % 2][ci][:, nb * 512:(nb + 1) * 512], start=(ci == 0), stop=(ci == 1)),
                     reads=[hk, 'WDN%d' % (jg % 2)], writes=['bank%d' % bk])
            P.op('dve', lambda e: e.tensor_tensor(out=X1[0:n, tt, nb * 512:(nb + 1) * 512], in0=X1[0:n, tt, nb * 512:(nb + 1) * 512], in1=banks[bk][0:n, :], op=ALU.add),
                 reads=['bank%d' % bk, 'X1_%d' % tt], writes=['X1_%d' % tt])

        NG = NJ // 2
        load_wup(0)
        pending = []
        for jg in range(NG):
            load_wdn(jg)
            ups = [(ci, bi) for ci in range(2) for bi in range(3)]
            per = (len(pending) + len(ups) - 1) // len(ups)
            for ui, (ci, bi) in enumerate(ups):
                if bi == 0:
                    jn = 2 * jg + ci + 1
                    if jn < NJ:
                        load_wup(jn)
                up_unit(jg, ci, bi)
                for _ in range(per):
                    if pending:
                        down_unit(*pending.pop(0))
            while pending:
                down_unit(*pending.pop(0))
            pending = [(jg, tt, nb) for tt in range(len(TILES)) for nb in range(4)]
        P.dma('sp', lambda e: e.dma_start(out=conv_out[:, :, :, :], in_=CONVOUT[:]), reads=['CONVOUT'], is_out=True)

        GFIN = V(0, [128, 2048], F32)
        YOUT = [V(2048 + i * 2048, [128, 2048], F32) for i in range(2)]
        P.op('dve', lambda e: e.memset(YOUT[0][:, 0:1], 0.0), writes=['YOUT0', 'YOUT1', 'GFIN', 'HN2T'])
        P.dma('sp', lambda e: e.dma_start(out=GFIN, in_=gfin_d[:, :]), writes=['GFIN'])

        def final_tile(tt):
            r0, n = TILES[tt]
            yo = YOUT[tt % 2]
            yk = 'YOUT%d' % (tt % 2)
            s_, k = rms_rstd(X1[0:n, tt, :], n, 'X1_%d' % tt, junk=yo, junkkey=yk)
            P.op('dve', lambda e: e.scalar_tensor_tensor(out=yo[0:n, :], in0=X1[0:n, tt, :], scalar=s_, in1=GFIN[0:n, :], op0=ALU.mult, op1=ALU.mult), reads=['X1_%d' % tt, k, 'GFIN'], writes=[yk])
            P.dma('sp', lambda e: e.dma_start(out=y_out[r0:r0 + n, :], in_=yo[0:n, :]), reads=[yk], is_out=True)

        while pending:
            u = pending.pop(0)
            down_unit(*u)
            if u[2] == 3:
                final_tile(u[1])
        P.finish()
    return nc


_NC = None


def kernel(x_prompt, x_sample, state_ssm_re, state_ssm_im, state_pool, state_ffn_conv,
           meta_tokens, g_mix, w_in, lam_re, lam_im, log_dt, b_re, b_im, c_re, c_im,
           d_skip, w_glu, w_pool, pool_scale, w_out, g_ffn, w_up, conv_w, conv_b,
           w_down, g_final):
    global _NC
    f = lambda a: np.ascontiguousarray(np.asarray(a, dtype=np.float32))
    x_prompt, x_sample = f(x_prompt), f(x_sample)
    meta = f(meta_tokens)
    B = x_prompt.shape[0]
    lam_pg = np.stack([f(lam_re)[0].T, f(lam_im)[0].T, np.broadcast_to(f(log_dt)[0][None, :], (64, 64))], axis=1)
    hp = lambda a: a.reshape(32, 2, 64).transpose(1, 2, 0).reshape(128, 32)
    ldt_hp = np.broadcast_to(f(log_dt)[0].reshape(32, 2).T[:, None, :], (2, 64, 32)).reshape(128, 32)
    lam_hp = np.stack([hp(f(lam_re)[0]), hp(f(lam_im)[0]), ldt_hp], axis=1)
    gm = lambda a: np.broadcast_to(a.reshape(8, 8, 1, 64).transpose(1, 2, 0, 3), (8, 16, 8, 64)).reshape(128, 512)
    ldt_gm = np.broadcast_to(f(log_dt)[0].reshape(8, 8, 1, 1).transpose(1, 2, 0, 3), (8, 16, 8, 64)).reshape(128, 512)
    lam_gm = np.stack([gm(f(lam_re)[0]), gm(f(lam_im)[0]), ldt_gm], axis=1)
    b_pg = np.stack([f(b_re)[0].transpose(1, 0, 2).reshape(64, 1024), f(b_im)[0].transpose(1, 0, 2).reshape(64, 1024)], axis=1)
    cpg = np.stack([f(c_re)[0].transpose(2, 0, 1).reshape(64, 1024), f(c_im)[0].transpose(2, 0, 1).reshape(64, 1024)], axis=1)
    c_hp = np.concatenate([cpg, cpg], axis=0)
    vecs = np.zeros((128, 224), np.float32)
    vecs[:, 0:16] = f(g_mix)[0].reshape(16, 128).T
    vecs[:, 16:32] = f(g_ffn)[0].reshape(16, 128).T
    vecs[:, 32:40] = f(d_skip)[0].reshape(8, 128).T
    vecs[:, 40:48] = f(pool_scale)[0].reshape(8, 128).T
    vecs[:, 48:180] = f(conv_w)[0].reshape(3, 44, 128).transpose(2, 0, 1).reshape(128, 132)
    vecs[:, 180:224] = f(conv_b)[0].reshape(44, 128).T
    gfin = np.broadcast_to(f(g_final)[None, :], (128, D))
    r = np.arange(128)
    maskB = np.zeros((128, 4, 2), np.float32)
    maskC = np.zeros((128, 4, 8), np.float32)
    for jj in range(4):
        for h in range(2):
            maskB[:, jj, h] = (r // 16 == 2 * jj + h)
        for gg in range(8):
            maskC[:, jj, gg] = (gg == 2 * jj + r // 64)
    masks = np.concatenate([maskB.reshape(128, 8), maskC.reshape(128, 32), -maskC.reshape(128, 32)], axis=1)
    shared = dict(lam_pg=f(lam_pg), lam_hp=f(lam_hp), lam_gm=f(lam_gm), b_pg=f(b_pg), c_hp=f(c_hp), vecs=vecs, gfin=f(gfin),
                  ident=np.eye(128, dtype=np.float32), masks=f(masks),
                  w_in=f(w_in)[0], w_glu=f(w_glu)[0], w_pool=f(w_pool)[0], w_out=f(w_out)[0],
                  w_up=f(w_up)[0], w_down=f(w_down)[0])
    sre, sim = f(state_ssm_re)[0], f(state_ssm_im)[0]
    spool, sconv = f(state_pool)[0], f(state_ffn_conv)[0]
    wins = np.array([2, 4, 8, 16], np.float32)
    in_maps = []
    for c in range(8):
        p, s = c // 4, c % 4
        seq = np.concatenate([meta, x_prompt[p]], axis=0)
        n = SEG * (s + 1)
        xall = np.zeros((PRE + E, D), np.float32)
        xall[PRE + WIN - n:PRE + WIN] = seq[:n]
        for i in range(NS):
            xall[PRE + WIN + 31 * i + 15:PRE + WIN + 31 * i + 31] = x_sample[NS * c + i]
        sl = slice(NS * c, NS * c + NS)
        t4 = lambda a: a.reshape(NS, 32, 2, 64).transpose(2, 3, 1, 0).reshape(128, 32, NS)
        st_ssm = np.stack([t4(sre[sl]).reshape(128, 4, 8, NS), t4(sim[sl]).reshape(128, 4, 8, NS)], axis=2).reshape(128, 64, NS)
        st_pool = spool[sl].reshape(NS, 15, 8, 128).transpose(3, 2, 0, 1)
        st_conv = sconv[sl].reshape(NS, 2, NJ, 128).transpose(3, 2, 0, 1)
        pos = SEG * s - OV + np.arange(WIN)
        cnt = np.empty((4, E), np.float32)
        for k in range(4):
            cnt[k, :WIN] = np.clip(np.minimum(pos + 1, wins[k]), 1, wins[k])
            cnt[k, WIN:] = wins[k]
        d = dict(shared)
        d.update(xall=xall, st_ssm=f(st_ssm), st_pool=f(st_pool), st_conv=f(st_conv),
                 cnt=f(np.broadcast_to(cnt[None], (128, 4, E))))
        in_maps.append(d)
    if _NC is None:
        _NC = build_nc()
    res = run_bass_kernel_spmd(_NC, in_maps, core_ids=list(range(8)))
    R = res.results
    S = x_prompt.shape[1]
    y_prompt = np.zeros((B, S, D), np.float32)
    y_sample = np.zeros((x_sample.shape[0], 16, D), np.float32)
    ssm_re_p = np.zeros((1, B, 64, 64), np.float32)
    ssm_im_p = np.zeros((1, B, 64, 64), np.float32)
    pool_p = np.zeros((1, B, 15, 1024), np.float32)
    conv_p = np.zeros((1, B, 2, DFF), np.float32)
    ssm_re_s = np.zeros((1, 32, 64, 64), np.float32)
    ssm_im_s = np.zeros((1, 32, 64, 64), np.float32)
    pool_s = np.zeros((1, 32, 15, 1024), np.float32)
    conv_s = np.zeros((1, 32, 2, DFF), np.float32)

    def ssm_unpack(a):
        a4 = a.reshape(128, 4, 2, 8)
        un = lambda x: x.reshape(2, 64, 32).transpose(2, 0, 1).reshape(64, 64)
        return un(a4[:, :, 0, :].reshape(128, 32)), un(a4[:, :, 1, :].reshape(128, 32))

    for c in range(8):
        p, s = c // 4, c % 4
        y = np.asarray(R[c]["y_out"], dtype=np.float32)
        so = np.asarray(R[c]["ssm_out"], dtype=np.float32)
        po = np.asarray(R[c]["pool_out"], dtype=np.float32)
        co = np.asarray(R[c]["conv_out"], dtype=np.float32)
        pos0 = SEG * s
        lo = max(pos0, 16)
        y_prompt[p, lo - 16:pos0 + SEG - 16] = y[OV + (lo - pos0):WIN]
        for i in range(NS):
            q = NS * c + i
            y_sample[q] = y[WIN + 31 * i + 15:WIN + 31 * i + 31]
            ssm_re_s[0, q], ssm_im_s[0, q] = ssm_unpack(so[:, :, i])
            pool_s[0, q] = po[:, :, i, :].transpose(2, 1, 0).reshape(15, 1024)
            conv_s[0, q] = co[:, :, i, :].transpose(2, 1, 0).reshape(2, DFF)
        if s == 3:
            ssm_re_p[0, p], ssm_im_p[0, p] = ssm_unpack(so[:, :, 4])
            pool_p[0, p] = po[:, :, 4, :].transpose(2, 1, 0).reshape(15, 1024)
            conv_p[0, p] = co[:, :, 4, :].transpose(2, 1, 0).reshape(2, DFF)
    return (y_prompt, y_sample, ssm_re_p, ssm_im_p, pool_p, conv_p,
            ssm_re_s, ssm_im_s, pool_s, conv_s)
```

```python
import numpy as np
from contextlib import ExitStack
import concourse.bass as bass
import concourse.mybir as mybir
from concourse.bass_utils import run_bass_kernel_spmd

F32 = mybir.dt.float32
BF16 = mybir.dt.bfloat16
I32 = mybir.dt.int32
AF = mybir.ActivationFunctionType
ALU = mybir.AluOpType

class Prog:
    def __init__(self, nc, es, ndma=48):
        self.nc = nc
        self.names = ['pe', 'act', 'dve', 'pool', 'sp']
        self.ops = {n: [] for n in self.names}
        self.sem = {n: es.enter_context(nc.semaphore('s_' + n)) for n in self.names}
        self.cnt = {n: 0 for n in self.names}
        self.ndma = ndma
        self.dsem = [es.enter_context(nc.semaphore('d%d' % i)) for i in range(ndma)]
        self.dcnt = [0] * ndma
        self.dnext = 0
        self.dnext_sw = ndma // 2
        self.waited = {n: {} for n in self.names}
        self.lastw = {}
        self.lastw_extra = {}
        self.readers = {}
        self.out_toks = []

    def _need(self, eng, tok):
        if tok is None:
            return
        kind, sid, val = tok
        if kind == 'e' and sid == eng:
            if eng in ('pe', 'sp') or self.cnt[eng] - val >= 6:
                return
        key = (kind, sid)
        if self.waited[eng].get(key, 0) >= val:
            return
        self.waited[eng][key] = val
        sem = self.sem[sid] if kind == 'e' else self.dsem[sid]
        self.ops[eng].append(lambda e, sem=sem, val=val: e.wait_ge(sem, val))

    def _deps(self, eng, reads, writes):
        for k in reads:
            self._need(eng, self.lastw.get(k))
            for t in self.lastw_extra.get(k, ()):
                self._need(eng, t)
        for k in writes:
            self._need(eng, self.lastw.get(k))
            for t in self.lastw_extra.get(k, ()):
                self._need(eng, t)
            for t in self.readers.get(k, {}).values():
                self._need(eng, t)

    def _commit(self, tok, reads, writes):
        for k in reads:
            d = self.readers.setdefault(k, {})
            key = (tok[0], tok[1])
            if key not in d or d[key][2] < tok[2]:
                d[key] = tok
        for k in writes:
            prev = self.lastw.get(k)
            if tok[0] == 'd' and prev is not None and prev[0] == 'd' and not self.readers.get(k):
                self.lastw_extra.setdefault(k, []).append(prev)
            else:
                self.lastw_extra[k] = []
            self.lastw[k] = tok
            self.readers[k] = {}

    def op(self, eng, fn, reads=(), writes=()):
        self._deps(eng, reads, writes)
        self.cnt[eng] += 1
        sem = self.sem[eng]
        self.ops[eng].append(lambda e, fn=fn, sem=sem: fn(e).then_inc(sem, 1))
        self._commit(('e', eng, self.cnt[eng]), reads, writes)

    def dma(self, q, fn, reads=(), writes=(), is_out=False):
        self._deps(q, reads, writes)
        half = self.ndma // 2
        if q == 'pool':
            i = self.dnext_sw
            self.dnext_sw = half + (i + 1 - half) % half
        else:
            i = self.dnext
            self.dnext = (i + 1) % half
        if self.dcnt[i] > 0:
            self._need(q, ('d', i, self.dcnt[i]))
        self.dcnt[i] += 16
        sem = self.dsem[i]
        self.ops[q].append(lambda e, fn=fn, sem=sem: fn(e).then_inc(sem, 16))
        tok = ('d', i, self.dcnt[i])
        self._commit(tok, reads, writes)
        if is_out:
            self.out_toks.append(tok)

    def finish(self):
        for tok in self.out_toks:
            self._need('sp', tok)
        for n in ['pe', 'act', 'dve', 'pool']:
            if self.cnt[n] > 0:
                self._need('sp', ('e', n, self.cnt[n]))
        nc = self.nc
        with nc.Block() as block:
            @block.tensor
            def _(e):
                for f in self.ops['pe']:
                    f(e)

            @block.scalar
            def _(e):
                for f in self.ops['act']:
                    f(e)

            @block.vector
            def _(e):
                for f in self.ops['dve']:
                    f(e)

            @block.gpsimd
            def _(e):
                for f in self.ops['pool']:
                    f(e)

            @block.sync
            def _(e):
                for f in self.ops['sp']:
                    f(e)

    def barrier(self):
        snap = dict(self.cnt)
        dsnap = list(self.dcnt)
        for x in self.names:
            for y in self.names:
                if y != x and snap[y] > 0:
                    self._need(x, ('e', y, snap[y]))
            if x not in ('pe', 'sp') and snap[x] > 0 and self.waited[x].get(('e', x), 0) < snap[x]:
                self.waited[x][('e', x)] = snap[x]
                sem, val = self.sem[x], snap[x]
                self.ops[x].append(lambda e, sem=sem, val=val: e.wait_ge(sem, val))
            for i in range(self.ndma):
                if dsnap[i] > 0:
                    self._need(x, ('d', i, dsnap[i]))


D = 2048
E = 1180
WIN = 1056
OV = 28
SEG = 1028
NS = 4
PRE = 3072
NPT = 24
DFF = 5632
NJ = 44
EPS = 1e-6
TILES = [(i * 128, min(128, E - i * 128)) for i in range(10)]
BLKS = [(0, 512), (512, 512), (1024, 156)]
SCOL = [WIN + 31 * i for i in range(NS)]
ARENA = 49152
TWO_PI = float(2 * np.pi)


def build_nc():
    nc = bass.Bass("TRN2", target_bir_lowering=False)
    din = lambda name, shape: nc.dram_tensor(name, list(shape), F32, kind="ExternalInput").ap()
    dout = lambda name, shape: nc.dram_tensor(name, list(shape), F32, kind="ExternalOutput").ap()
    xall = din("xall", [PRE + E, D])
    lam_pg = din("lam_pg", [64, 3, 64])
    lam_hp = din("lam_hp", [128, 3, 32])
    lam_gm = din("lam_gm", [128, 3, 512])
    b_pg = din("b_pg", [64, 2, 1024])
    c_hp = din("c_hp", [128, 2, 1024])
    st_ssm = din("st_ssm", [128, 64, NS])
    st_pool = din("st_pool", [128, 8, NS, 15])
    st_conv = din("st_conv", [128, NJ, NS, 2])
    vecs_d = din("vecs", [128, 224])
    gfin_d = din("gfin", [128, D])
    cnt_d = din("cnt", [128, 4, E])
    ident_d = din("ident", [128, 128])
    masks_d = din("masks", [128, 72])
    w_in = din("w_in", [D, D])
    w_glu = din("w_glu", [1024, 1024])
    w_pool = din("w_pool", [4, 256, 256])
    w_out = din("w_out", [D, D])
    w_up = din("w_up", [D, 2 * DFF])
    w_down = din("w_down", [DFF, D])
    y_out = dout("y_out", [E, D])
    ssm_out = dout("ssm_out", [128, 64, 5])
    pool_out = dout("pool_out", [128, 8, 5, 15])
    conv_out = dout("conv_out", [128, NJ, 5, 2])

    with ExitStack() as es:
        P = Prog(nc, es)
        sbt = lambda name, shape, dt=F32: es.enter_context(nc.sbuf_tensor("sb_" + name, shape, dt))
        arena = sbt("arena", [128, ARENA], F32)
        ident = sbt("ident", [128, 128], F32)
        identb = sbt("identb", [128, 128], BF16)
        vecs = sbt("vecs", [128, 224], F32)
        masks = sbt("masks", [128, 72], F32)
        A1 = sbt("A1", [128, 64], F32)
        A2 = sbt("A2", [128, 64], F32)
        A1_, A2_ = A1, A2
        TMP = sbt("TMP", [128, 512], F32)
        TMP2 = sbt("TMP2", [128, 512], F32)
        A16_1 = sbt("A16_1", [128, 64], F32)
        A16_2 = sbt("A16_2", [128, 64], F32)
        HPRE = sbt("HPRE", [128, 64], F32)
        P16 = sbt("P16", [128, 5, 32], F32)
        SSMOUT = sbt("SSMOUT", [128, 64, 5], F32)
        POOLOUT = sbt("POOLOUT", [128, 8, 5, 15], F32)
        CONVOUT = sbt("CONVOUT", [128, NJ, 5, 2], F32)
        STS = sbt("STS", [128, 64, NS], F32)
        STC = sbt("STC", [128, NJ, NS, 2], F32)
        small = sbt("small", [128, 16], F32)
        epsb = sbt("epsb", [128, 1], F32)
        PS = es.enter_context(nc.psum_tensor("PS", [128, 4096], F32))
        banks = [PS[:, 512 * i:512 * (i + 1)] for i in range(8)]

        gmix = vecs[:, 0:16]
        gffn = vecs[:, 16:32]
        dsk = vecs[:, 32:40]
        psc = vecs[:, 40:48]
        cw = vecs[:, 48:180].rearrange("p (a b) -> p a b", a=3)
        cb = vecs[:, 180:224]
        maskB = masks[:, 0:8].rearrange("p (a b) -> p a b", a=4)
        maskC = masks[:, 8:40].rearrange("p (a b) -> p a b", a=4)
        maskCn = masks[:, 40:72].rearrange("p (a b) -> p a b", a=4)

        def V(off, shape, dt=F32, parts=128):
            n = int(np.prod(shape[1:]))
            w = n if dt != BF16 else (n + 1) // 2
            assert off + w <= ARENA, (off, w)
            ap = arena[0:parts, off:off + w]
            if dt != F32:
                ap = ap.bitcast(dt)
            if len(shape) == 3:
                ap = ap.rearrange("p (a b) -> p a b", a=shape[1])
            elif len(shape) == 4:
                ap = ap.rearrange("p (a b c) -> p a b c", a=shape[1], b=shape[2])
            return ap

        P.dma('sp', lambda e: e.dma_start(out=ident[:], in_=ident_d[:, :]), writes=['ident'])
        P.dma('sp', lambda e: e.dma_start(out=vecs[:], in_=vecs_d[:, :]), writes=['vecs'])
        P.dma('sp', lambda e: e.dma_start(out=masks[:], in_=masks_d[:, :]), writes=['masks'])
        P.dma('sp', lambda e: e.dma_start(out=STS[:], in_=st_ssm[:, :, :]), writes=['STS'])
        P.dma('sp', lambda e: e.dma_start(out=STC[:], in_=st_conv[:, :, :, :]), writes=['STC'])
        P.op('dve', lambda e: e.tensor_copy(identb[:], ident[:]), reads=['ident'], writes=['identb'])
        P.op('pool', lambda e: e.memset(epsb[:], EPS), writes=['epsb'])
        P.op('pool', lambda e: e.memset(SSMOUT[:], 0.0), writes=['SSMOUT'])

        BPAD = V(0, [128, 64, 128], BF16)
        CPAD = V(4096, [128, 64, 128], BF16)

        T0 = 16448

        def lambar(np_, ncol, src, base):
            t = {}
            names = ['raw0', 'raw1', 'raw2', 'dt', 'a', 'th', 'mag', 'cs', 'sn', 'lbr', 'lbi', 'w1', 'w2', 'w3']
            for i, nm in enumerate(names):
                t[nm] = V(base + i * ncol, [np_, ncol], F32, parts=np_)
            ti = V(base + len(names) * ncol, [np_, ncol], I32, parts=np_)
            raw = V(base, [np_, 3, ncol], F32, parts=np_)
            kp = 'lb%d_' % np_
            P.dma('sp', lambda e: e.dma_start(out=raw, in_=src[:, :, :]), writes=[kp + 'raw'])
            lr, li = t['raw0'], t['raw1']
            P.op('act', lambda e: e.activation(out=t['dt'], in_=t['raw2'], func=AF.Exp), reads=[kp + 'raw'], writes=[kp + 'dt'])
            P.op('dve', lambda e: e.tensor_tensor(out=t['a'], in0=lr, in1=t['dt'], op=ALU.mult), reads=[kp + 'raw', kp + 'dt'], writes=[kp + 'a'])
            P.op('dve', lambda e: e.tensor_tensor(out=t['th'], in0=li, in1=t['dt'], op=ALU.mult), reads=[kp + 'raw', kp + 'dt'], writes=[kp + 'th'])
            P.op('act', lambda e: e.activation(out=t['mag'], in_=t['a'], func=AF.Exp), reads=[kp + 'a'], writes=[kp + 'mag'])

            def sin_of(dst, shift):
                w1, w2, w3 = t['w1'], t['w2'], t['w3']
                k = kp + 'w'
                P.op('dve', lambda e: e.tensor_scalar(out=w1, in0=t['th'], scalar1=float(shift), scalar2=None, op0=ALU.add), reads=[kp + 'th'], writes=[k])
                P.op('dve', lambda e: e.tensor_scalar(out=w2, in0=w1, scalar1=1.0 / TWO_PI, scalar2=None, op0=ALU.mult), reads=[k], writes=[k])
                P.op('dve', lambda e: e.tensor_copy(ti, w2), reads=[k], writes=[k])
                P.op('dve', lambda e: e.tensor_copy(w2, ti), reads=[k], writes=[k])
                P.op('dve', lambda e: e.scalar_tensor_tensor(out=w1, in0=w2, scalar=-TWO_PI, in1=w1, op0=ALU.mult, op1=ALU.add), reads=[k], writes=[k])
                P.op('dve', lambda e: e.tensor_scalar(out=w3, in0=w1, scalar1=float(np.pi), scalar2=-TWO_PI, op0=ALU.is_gt, op1=ALU.mult), reads=[k], writes=[k])
                P.op('dve', lambda e: e.tensor_tensor(out=w1, in0=w1, in1=w3, op=ALU.add), reads=[k], writes=[k])
                P.op('dve', lambda e: e.tensor_scalar(out=w3, in0=w1, scalar1=float(-np.pi), scalar2=TWO_PI, op0=ALU.is_lt, op1=ALU.mult), reads=[k], writes=[k])
                P.op('dve', lambda e: e.tensor_tensor(out=w1, in0=w1, in1=w3, op=ALU.add), reads=[k], writes=[k])
                P.op('dve', lambda e: e.tensor_scalar(out=w1, in0=w1, scalar1=float(-np.pi), scalar2=float(np.pi), op0=ALU.max, op1=ALU.min), reads=[k], writes=[k])
                P.op('act', lambda e: e.activation(out=dst, in_=w1, func=AF.Sin), reads=[k], writes=[k, kp + 'sc'])

            sin_of(t['sn'], 0.0)
            sin_of(t['cs'], np.pi / 2)
            P.op('dve', lambda e: e.tensor_tensor(out=t['lbr'], in0=t['mag'], in1=t['cs'], op=ALU.mult), reads=[kp + 'mag', kp + 'sc'], writes=[kp + 'lb'])
            P.op('dve', lambda e: e.tensor_tensor(out=t['lbi'], in0=t['mag'], in1=t['sn'], op=ALU.mult), reads=[kp + 'mag', kp + 'sc'], writes=[kp + 'lb'])
            return t, kp

        th_, kp = lambar(128, 32, lam_hp, T0)
        A1v = A1[:, :].rearrange("p (q c j) -> p q c j", q=4, c=2)
        A2v_ = A2[:, :].rearrange("p (q c j) -> p q c j", q=4, c=2)
        A161v = A16_1[:, :].rearrange("p (q c j) -> p q c j", q=4, c=2)
        A162v = A16_2[:, :].rearrange("p (q c j) -> p q c j", q=4, c=2)
        q4 = lambda ap: ap.rearrange("p (q j) -> p q j", q=4)
        for c in range(2):
            P.op('dve', lambda e, c=c: e.tensor_copy(A1v[:, :, c, :], q4(th_['lbr'])), reads=[kp + 'lb'], writes=['A'])
        P.op('dve', lambda e: e.tensor_copy(A2v_[:, :, 1, :], q4(th_['lbi'])), reads=[kp + 'lb'], writes=['A'])
        P.op('dve', lambda e: e.tensor_scalar(out=A2v_[:, :, 0, :], in0=q4(th_['lbi']), scalar1=-1.0, scalar2=None, op0=ALU.mult), reads=[kp + 'lb'], writes=['A'])
        pr, pi_, s1, s2, s3 = (P16[:, i, :] for i in range(5))
        P.op('dve', lambda e: e.tensor_copy(pr, th_['lbr']), reads=[kp + 'lb'], writes=['P16'])
        P.op('dve', lambda e: e.tensor_copy(pi_, th_['lbi']), reads=[kp + 'lb'], writes=['P16'])
        A1h = TMP[:, 256:320]
        A2h = TMP[:, 320:384]
        A1hv = A1h.rearrange("p (q c j) -> p q c j", q=4, c=2)
        A2hv = A2h.rearrange("p (q c j) -> p q c j", q=4, c=2)
        for it_ in range(4):
            if it_ == 1:
                for c in range(2):
                    P.op('dve', lambda e, c=c: e.tensor_copy(A1hv[:, :, c, :], q4(pr)), reads=['P16'], writes=['Ah'])
                P.op('dve', lambda e: e.tensor_copy(A2hv[:, :, 1, :], q4(pi_)), reads=['P16'], writes=['Ah'])
                P.op('dve', lambda e: e.tensor_scalar(out=A2hv[:, :, 0, :], in0=q4(pi_), scalar1=-1.0, scalar2=None, op0=ALU.mult), reads=['P16'], writes=['Ah'])
            P.op('dve', lambda e: e.tensor_tensor(out=s1, in0=pr, in1=pr, op=ALU.mult), reads=['P16', 'Ah'] if it_ == 1 else ['P16'], writes=['P16'])
            P.op('dve', lambda e: e.tensor_tensor(out=s2, in0=pi_, in1=pi_, op=ALU.mult), reads=['P16'], writes=['P16'])
            P.op('dve', lambda e: e.tensor_tensor(out=s3, in0=pr, in1=pi_, op=ALU.mult), reads=['P16'], writes=['P16'])
            P.op('dve', lambda e: e.tensor_tensor(out=pr, in0=s1, in1=s2, op=ALU.subtract), reads=['P16'], writes=['P16'])
            P.op('dve', lambda e: e.tensor_scalar(out=pi_, in0=s3, scalar1=2.0, scalar2=None, op0=ALU.mult), reads=['P16'], writes=['P16'])
        for c in range(2):
            P.op('dve', lambda e, c=c: e.tensor_copy(A161v[:, :, c, :], q4(pr)), reads=['P16'], writes=['A16'])
        P.op('dve', lambda e: e.tensor_copy(A162v[:, :, 1, :], q4(pi_)), reads=['P16'], writes=['A16'])
        P.op('dve', lambda e: e.tensor_scalar(out=A162v[:, :, 0, :], in0=q4(pi_), scalar1=-1.0, scalar2=None, op0=ALU.mult), reads=['P16'], writes=['A16'])
        T1 = T0 + 15 * 32 + 32
        tp, kq = lambar(64, 64, lam_pg, T1)
        T2 = T1 + 16 * 64
        fr = V(T2, [64, 64], F32, parts=64)
        fi = V(T2 + 64, [64, 64], F32, parts=64)
        den = V(T2 + 128, [64, 64], F32, parts=64)
        nr = V(T2 + 192, [64, 64], F32, parts=64)
        q1 = V(T2 + 256, [64, 64], F32, parts=64)
        q2 = V(T2 + 320, [64, 64], F32, parts=64)
        lr, li, lbr, lbi = tp['raw0'], tp['raw1'], tp['lbr'], tp['lbi']
        kf = 'ftab'
        P.op('dve', lambda e: e.tensor_scalar(out=nr, in0=lbr, scalar1=-1.0, scalar2=None, op0=ALU.add), reads=[kq + 'lb'], writes=[kf])
        P.op('dve', lambda e: e.tensor_tensor(out=q1, in0=lr, in1=lr, op=ALU.mult), reads=[kq + 'raw'], writes=[kf])
        P.op('dve', lambda e: e.tensor_tensor(out=q2, in0=li, in1=li, op=ALU.mult), reads=[kq + 'raw'], writes=[kf])
        P.op('dve', lambda e: e.tensor_tensor(out=den, in0=q1, in1=q2, op=ALU.add), reads=[kf], writes=[kf])
        P.op('dve', lambda e: e.reciprocal(den, den), reads=[kf], writes=[kf])
        P.op('dve', lambda e: e.tensor_tensor(out=q1, in0=nr, in1=lr, op=ALU.mult), reads=[kf], writes=[kf])
        P.op('dve', lambda e: e.tensor_tensor(out=q2, in0=lbi, in1=li, op=ALU.mult), reads=[kf, kq + 'lb'], writes=[kf])
        P.op('dve', lambda e: e.tensor_tensor(out=q1, in0=q1, in1=q2, op=ALU.add), reads=[kf], writes=[kf])
        P.op('dve', lambda e: e.tensor_tensor(out=fr, in0=q1, in1=den, op=ALU.mult), reads=[kf], writes=[kf])
        P.op('dve', lambda e: e.tensor_tensor(out=q1, in0=lbi, in1=lr, op=ALU.mult), reads=[kf, kq + 'lb'], writes=[kf])
        P.op('dve', lambda e: e.tensor_tensor(out=q2, in0=nr, in1=li, op=ALU.mult), reads=[kf], writes=[kf])
        P.op('dve', lambda e: e.tensor_tensor(out=q1, in0=q1, in1=q2, op=ALU.subtract), reads=[kf], writes=[kf])
        P.op('dve', lambda e: e.tensor_tensor(out=fi, in0=q1, in1=den, op=ALU.mult), reads=[kf], writes=[kf])
        T3 = T2 + 384
        Braw = V(T3, [64, 2, 64, 16], F32, parts=64)
        Bbar = V(T3 + 2048, [64, 2, 1024], F32, parts=64)
        Bt1 = V(T3 + 4096, [64, 64, 16], F32, parts=64)
        Bt2 = V(T3 + 5120, [64, 64, 16], F32, parts=64)
        Bc = V(T3 + 6144, [128, 8, 2, 64], F32)
        P.dma('sp', lambda e: e.dma_start(out=V(T3, [64, 2, 1024], F32, parts=64), in_=b_pg[:, :, :]), writes=['Braw'])
        frb = fr.unsqueeze(2).to_broadcast([64, 64, 16])
        fib = fi.unsqueeze(2).to_broadcast([64, 64, 16])
        Bbar4 = V(T3 + 2048, [64, 2, 64, 16], F32, parts=64)
        P.op('pool', lambda e: e.tensor_tensor(out=Bt1, in0=Braw[:, 0], in1=frb, op=ALU.mult), reads=['Braw', kf], writes=['Bt1'])
        P.op('pool', lambda e: e.tensor_tensor(out=Bt2, in0=Braw[:, 1], in1=fib, op=ALU.mult), reads=['Braw', kf], writes=['Bt2'])
        P.op('pool', lambda e: e.tensor_tensor(out=Bbar4[:, 0], in0=Bt1, in1=Bt2, op=ALU.subtract), reads=['Bt1', 'Bt2'], writes=['Bbar'])
        P.op('pool', lambda e: e.tensor_tensor(out=Bt1, in0=Braw[:, 1], in1=frb, op=ALU.mult), reads=['Braw', kf, 'Bbar'], writes=['Bt1'])
        P.op('pool', lambda e: e.tensor_tensor(out=Bt2, in0=Braw[:, 0], in1=fib, op=ALU.mult), reads=['Braw', kf, 'Bbar'], writes=['Bt2'])
        P.op('pool', lambda e: e.tensor_tensor(out=Bbar4[:, 1], in0=Bt1, in1=Bt2, op=ALU.add), reads=['Bt1', 'Bt2'], writes=['Bbar'])
        for k in range(8):
            for c in range(2):
                bk = banks[(2 * k + c) % 2]
                key = 'bank%d' % ((2 * k + c) % 2)
                P.op('pe', lambda e, k=k, c=c, bk=bk: e.transpose(bk[:, 0:64], Bbar[:, c, k * 128:(k + 1) * 128], ident[0:64, 0:64]), reads=['Bbar', 'ident'], writes=[key])
                P.op('act', lambda e, k=k, c=c, bk=bk: e.activation(out=Bc[:, k, c, :], in_=bk[:, 0:64], func=AF.Copy), reads=[key], writes=['Bc'])
        for j in range(32):
            for c in range(2):
                P.op('pool', lambda e, j=j, c=c: e.tensor_tensor(
                    out=BPAD[:, c * 32 + j, :].rearrange("p (a b) -> p a b", a=2),
                    in0=Bc[:, j // 4, c, :].unsqueeze(1).to_broadcast([128, 2, 64]),
                    in1=maskB[:, j % 4, :].unsqueeze(2).to_broadcast([128, 2, 64]), op=ALU.mult),
                    reads=['Bc', 'masks'], writes=['BPAD'])
        tg, kg = lambar(128, 512, lam_gm, 28000)
        lgr = tg['lbr'].rearrange("p (k q) -> p k q", k=8)
        lgi = tg['lbi'].rearrange("p (k q) -> p k q", k=8)
        Bc1 = V(36000, [128, 8, 2, 64], F32)
        Bm1 = V(37024, [128, 8, 64], F32)
        Bm2 = V(37536, [128, 8, 64], F32)
        P.op('dve', lambda e: e.tensor_tensor(out=Bm1, in0=lgr, in1=Bc[:, :, 0, :], op=ALU.mult), reads=[kg + 'lb', 'Bc'], writes=['Bm1'])
        P.op('dve', lambda e: e.tensor_tensor(out=Bm2, in0=lgi, in1=Bc[:, :, 1, :], op=ALU.mult), reads=[kg + 'lb', 'Bc'], writes=['Bm2'])
        P.op('dve', lambda e: e.tensor_tensor(out=Bc1[:, :, 0, :], in0=Bm1, in1=Bm2, op=ALU.subtract), reads=['Bm1', 'Bm2'], writes=['Bc1'])
        P.op('dve', lambda e: e.tensor_tensor(out=Bm1, in0=lgr, in1=Bc[:, :, 1, :], op=ALU.mult), reads=[kg + 'lb', 'Bc', 'Bc1'], writes=['Bm1'])
        P.op('dve', lambda e: e.tensor_tensor(out=Bm2, in0=lgi, in1=Bc[:, :, 0, :], op=ALU.mult), reads=[kg + 'lb', 'Bc', 'Bc1'], writes=['Bm2'])
        P.op('dve', lambda e: e.tensor_tensor(out=Bc1[:, :, 1, :], in0=Bm1, in1=Bm2, op=ALU.add), reads=['Bm1', 'Bm2'], writes=['Bc1'])
        BPAD1 = V(4096, [128, 64, 128], BF16)
        for j in range(32):
            for c in range(2):
                P.op('pool', lambda e, j=j, c=c: e.tensor_tensor(
                    out=BPAD1[:, c * 32 + j, :].rearrange("p (a b) -> p a b", a=2),
                    in0=Bc1[:, j // 4, c, :].unsqueeze(1).to_broadcast([128, 2, 64]),
                    in1=maskB[:, j % 4, :].unsqueeze(2).to_broadcast([128, 2, 64]), op=ALU.mult),
                    reads=['Bc1', 'masks'], writes=['BPAD1'])
        P.barrier()

        NCH_H = 64
        NCH_W = WIN // 16
        VALL = V(8192, [128, 64, NCH_H], F32)
        TMPQ = [V(12288 + i * 1024, [128, 16, NCH_H], F32) for i in range(2)]
        VTS = [V(14336 + i * 4160, [128, 1 + NCH_H, 64], F32) for i in range(2)]
        UALLP = [V(22656 + i * 4096, [128, 8, 16 * NCH_H], BF16) for i in range(2)]
        WIN_SSM = V(30848, [128, 16, 1024], BF16)
        HNTP = V(39040, [128, 16, 512], BF16)
        XT = [V(43136 + i * 2048, [128, 2048], F32) for i in range(2)]
        XS = [V(47232, [128, 2048], BF16) for i in range(2)]
        SQJ = XS[0]
        WST = [V(40496 + i * 1024, [128, 16, 128], BF16) for i in range(2)]
        bank_bf = [b.bitcast(BF16) for b in banks]
        cnt_small = [0]

        def std_jc(jcp):
            q, r_ = jcp // 16, jcp % 16
            c, jl = r_ // 8, r_ % 8
            return c * 32 + 8 * q + jl, 8 * q + jl

        def rms_rstd(src, n, srckey, junk=None, junkkey='SQJ'):
            junk = SQJ if junk is None else junk
            i = cnt_small[0] % 16
            cnt_small[0] += 1
            s = small[0:n, i:i + 1]
            k = 'small%d' % i
            P.op('act', lambda e: e.activation(out=junk[0:n, :], in_=src, func=AF.Square, accum_out=s), reads=[srckey], writes=[k, junkkey])
            P.op('act', lambda e: e.activation(out=s, in_=s, func=AF.Sqrt, bias=epsb[0:n, :], scale=1.0 / D), reads=[k, 'epsb'], writes=[k])
            P.op('dve', lambda e: e.reciprocal(s, s), reads=[k], writes=[k])
            return s, k

        def norm_transpose(src, n, srckey, gvec, dst, dstkey, c0, slot, xsbuf=None, extra_w=()):
            xs = XS[slot] if xsbuf is None else xsbuf
            xk = 'XS' if xsbuf is None else 'XSF'
            s, k = rms_rstd(src, n, srckey, junk=xs, junkkey=xk)
            P.op('act', lambda e: e.activation(out=xs[0:n, :], in_=src, func=AF.Identity, scale=s), reads=[srckey, k], writes=[xk])
            for kk in range(16):
                b = kk // 8
                P.op('pe', lambda e, kk=kk, b=b: e.transpose(bank_bf[b][:, (kk % 8) * 128:(kk % 8) * 128 + n], xs[0:n, kk * 128:(kk + 1) * 128], identb[0:n, 0:n]),
                     reads=[xk, 'identb'], writes=['bank%d' % b])
            for b in range(2):
                P.op('dve', lambda e, b=b: e.tensor_tensor(out=dst[:, 8 * b:8 * b + 8, c0:c0 + n], in0=bank_bf[b].rearrange("p (k t) -> p k t", k=8)[:, :, 0:n],
                                                          in1=gvec[:, 8 * b:8 * b + 8].unsqueeze(2).to_broadcast([128, 8, n]), op=ALU.mult),
                     reads=['bank%d' % b, 'vecs'], writes=[dstkey] + list(extra_w))

        PSQ = PS[:, 2048:4096].rearrange("p (a b) -> p a b", a=16)

        def bu_quarter(q, rhs_of, n, ukey, rhs_of1=None):
            for i in range(16):
                jc, j = std_jc(16 * q + i)
                bk = 4 + i // 4
                rhs = rhs_of(j // 4)
                if rhs_of1 is not None:
                    rhs1 = rhs_of1(j // 4)
                    P.op('pe', lambda e, i=i, jc=jc, rhs1=rhs1: e.matmul(PSQ[:, i, 0:n], BPAD1[:, jc, :], rhs1, start=True, stop=False),
                         reads=['BPAD1', ukey], writes=['bank%d' % bk])
                P.op('pe', lambda e, i=i, jc=jc, rhs=rhs: e.matmul(PSQ[:, i, 0:n], BPAD[:, jc, :], rhs, start=(rhs_of1 is None), stop=True),
                     reads=['BPAD', ukey], writes=['bank%d' % bk])

        PSQK = ['bank4', 'bank5', 'bank6', 'bank7']

        def run(*gens):
            gens = list(gens)
            while gens:
                for g in list(gens):
                    try:
                        next(g)
                    except StopIteration:
                        gens.remove(g)

        def horner_gen(rhs_fn, ukey, nch, vall, tmpqs, vkp='VALL', two_step=False):
            A1, A2 = (A1h, A2h) if two_step else (A1_, A2_)
            akey = 'Ah' if two_step else 'A'
            for s in range(8 if two_step else 16):
                for q in range(4):
                    tmpq = tmpqs[q % 2]
                    tk = 'TMPQ%d' % (q % 2)
                    vq = vall[:, 16 * q:16 * q + 16, 0:nch]
                    vre = vall[:, 16 * q:16 * q + 8, 0:nch]
                    vim = vall[:, 16 * q + 8:16 * q + 16, 0:nch]
                    a1 = A1[:, 16 * q:16 * q + 16].unsqueeze(2).to_broadcast([128, 16, nch])
                    a2re = A2[:, 16 * q:16 * q + 8].unsqueeze(2).to_broadcast([128, 8, nch])
                    a2im = A2[:, 16 * q + 8:16 * q + 16].unsqueeze(2).to_broadcast([128, 8, nch])
                    vk = vkp + '%d' % q
                    if s > 0:
                        P.op('pool', lambda e, a2re=a2re, vim=vim, tmpq=tmpq: e.tensor_tensor(out=tmpq[:, 0:8, 0:nch], in0=a2re, in1=vim, op=ALU.mult), reads=[vk, akey], writes=[tk + 'a'])
                        P.op('pool', lambda e, a2im=a2im, vre=vre, tmpq=tmpq: e.tensor_tensor(out=tmpq[:, 8:16, 0:nch], in0=a2im, in1=vre, op=ALU.mult), reads=[vk, akey], writes=[tk + 'b'])
                    if two_step:
                        bu_quarter(q, lambda k, s=s: rhs_fn(k, 2 * s + 1), nch, ukey, rhs_of1=lambda k, s=s: rhs_fn(k, 2 * s))
                    else:
                        bu_quarter(q, lambda k, s=s: rhs_fn(k, s), nch, ukey)
                    if s == 0:
                        P.op('act', lambda e, vq=vq: e.activation(out=vq, in_=PSQ[:, :, 0:nch], func=AF.Copy), reads=PSQK, writes=[vk])
                    else:
                        P.op('dve', lambda e, vq=vq, a1=a1: e.tensor_tensor(out=vq, in0=vq, in1=a1, op=ALU.mult), reads=[akey, tk + 'a', tk + 'b'], writes=[vk])
                        P.op('dve', lambda e, vq=vq, tmpq=tmpq: e.tensor_tensor(out=vq, in0=vq, in1=tmpq[:, :, 0:nch], op=ALU.add), reads=[tk + 'a', tk + 'b'], writes=[vk])
                        P.op('dve', lambda e, vq=vq: e.tensor_tensor(out=vq, in0=vq, in1=PSQ[:, :, 0:nch], op=ALU.add), reads=PSQK, writes=[vk])
                    yield

        TMPs = TMP[:, 448:512]
        TMP2f = TMP2[:, 448:512]
        TMP2s = TMP2f.rearrange("p (q c j) -> p q c j", q=4, c=2)
        A16_2v = A16_2[:, :].rearrange("p (q c j) -> p q c j", q=4, c=2)

        def chunk_scan_gen(vt, nch, vtk, e2='pool'):
            for c in range(nch):
                prev = vt[:, c, :]
                prevv = prev.rearrange("p (q c j) -> p q c j", q=4, c=2)
                cur = vt[:, c + 1, :]
                P.op(e2, lambda e, prevv=prevv: e.tensor_tensor(out=TMP2s[:, :, 0, :], in0=A16_2v[:, :, 0, :], in1=prevv[:, :, 1, :], op=ALU.mult), reads=[vtk, 'A16'], writes=['TMP2ca'])
                P.op(e2, lambda e, prevv=prevv: e.tensor_tensor(out=TMP2s[:, :, 1, :], in0=A16_2v[:, :, 1, :], in1=prevv[:, :, 0, :], op=ALU.mult), reads=[vtk, 'A16'], writes=['TMP2cb'])
                P.op('dve', lambda e, prev=prev: e.tensor_tensor(out=TMPs, in0=A16_1[:, :], in1=prev, op=ALU.mult), reads=[vtk, 'A16'], writes=['TMPc'])
                P.op('dve', lambda e, cur=cur: e.tensor_tensor(out=cur, in0=cur, in1=TMPs, op=ALU.add), reads=['TMPc'], writes=[vtk])
                P.op('dve', lambda e, cur=cur: e.tensor_tensor(out=cur, in0=cur, in1=TMP2f, op=ALU.add), reads=['TMP2ca', 'TMP2cb'], writes=[vtk])
                yield

        for m in range(8):
            P.dma('pool', lambda e, m=m: e.dma_start(out=WIN_SSM[:, :, m * 128:(m + 1) * 128], in_=w_in[:, m * 128:(m + 1) * 128].rearrange("(k p) m -> p k m", p=128)), writes=['WIN_SSM'])
        NPART = PRE // (16 * NCH_H)

        def front_gen(p):
            ua = UALLP[p % 2]
            uk = 'UALLP%d' % (p % 2)
            for tl in range(8):
                ti = p * 8 + tl
                slot = ti % 2
                P.dma('sp', lambda e, ti=ti, slot=slot: e.dma_start(out=XT[slot][:, :], in_=xall[ti * 128:(ti + 1) * 128, :]), writes=['XT%d' % slot])
                norm_transpose(XT[slot][:, :], 128, 'XT%d' % slot, gmix, HNTP, 'HNTP', (ti % 4) * 128, slot)
                yield
                if ti % 4 == 3:
                    blk = tl // 4
                    for m in range(8):
                        bk = 2 + m % 2
                        for kk in range(16):
                            P.op('pe', lambda e, m=m, kk=kk, bk=bk: e.matmul(banks[bk][:, :], WIN_SSM[:, kk, m * 128:(m + 1) * 128], HNTP[:, kk, :], start=(kk == 0), stop=(kk == 15)),
                                 reads=['WIN_SSM', 'HNTP'], writes=['bank%d' % bk])
                        P.op('act', lambda e, m=m, bk=bk, blk=blk, ua=ua: e.activation(out=ua[:, m, blk * 512:(blk + 1) * 512], in_=banks[bk][:, :], func=AF.Copy), reads=['bank%d' % bk], writes=[uk])
                        yield

        def horner_part_gen(p):
            uav = UALLP[p % 2].rearrange("p k (c s) -> p k c s", s=16)
            vt = VTS[p % 2]
            yield from horner_gen(lambda k, s: uav[:, k, :, s], 'UALLP%d' % (p % 2), NCH_H, VALL, TMPQ, two_step=True)
            P.op('pool', lambda e, vt=vt: e.tensor_copy(vt[:, 1:1 + NCH_H, :].rearrange("p c j -> p j c"), VALL[:, :, :]), reads=['VALL0', 'VALL1', 'VALL2', 'VALL3'], writes=['vt%d' % (p % 2)])
            yield

        def cscan_part_gen(p):
            vt = VTS[p % 2]
            vtk = 'vt%d' % (p % 2)
            if p == 0:
                P.op('dve', lambda e, vt=vt: e.memset(vt[:, 0, :], 0.0), writes=[vtk])
            else:
                pv = VTS[(p - 1) % 2]
                P.op('dve', lambda e, vt=vt, pv=pv: e.tensor_copy(vt[:, 0, :], pv[:, NCH_H, :]), reads=['vt%d' % ((p - 1) % 2)], writes=[vtk])
            yield
            yield from chunk_scan_gen(vt, NCH_H, vtk, e2=('dve' if p == NPART - 1 else 'pool'))

        for st in range(NPART + 2):
            gens = []
            if st < NPART:
                gens.append(front_gen(st))
            if 0 <= st - 1 < NPART:
                gens.append(horner_part_gen(st - 1))
            if 0 <= st - 2 < NPART:
                gens.append(cscan_part_gen(st - 2))
            run(*gens)
        P.op('dve', lambda e: e.tensor_copy(HPRE[:, :], VTS[(NPART - 1) % 2][:, NCH_H, :]), reads=['vt%d' % ((NPART - 1) % 2)], writes=['HPRE'])
        P.barrier()

        HBW = V(8192, [128, 17, 64, 8], F32)
        USB = V(16896, [128, 8, E], BF16)
        Z = V(21616, [128, 8, E], F32)
        HNT = V(31056, [128, 16, E], BF16)
        for tt, (r0, n) in enumerate(TILES):
            slot = tt % 2
            P.dma('sp', lambda e, r0=r0, n=n, slot=slot: e.dma_start(out=XT[slot][0:n, :], in_=xall[PRE + r0:PRE + r0 + n, :]), writes=['XT%d' % slot])
            norm_transpose(XT[slot][0:n, :], n, 'XT%d' % slot, gmix, HNT, 'HNT', r0, slot)
        def m2_gen(ms):
            for m in ms:
                ws = m % 2
                P.dma('pool', lambda e, m=m, ws=ws: e.dma_start(out=WST[ws][:, :, :], in_=w_in[:, m * 128:(m + 1) * 128].rearrange("(k p) m -> p k m", p=128)), writes=['WST%d' % ws])
                for bi, (c0, n) in enumerate(BLKS):
                    bk = 2 + (m * 3 + bi) % 2
                    for kk in range(16):
                        P.op('pe', lambda e, kk=kk, bk=bk, ws=ws, c0=c0, n=n: e.matmul(banks[bk][:, 0:n], WST[ws][:, kk, :], HNT[:, kk, c0:c0 + n], start=(kk == 0), stop=(kk == 15)),
                             reads=['WST%d' % ws, 'HNT'], writes=['bank%d' % bk])
                    if m < 8:
                        P.op('act', lambda e, m=m, bk=bk, c0=c0, n=n: e.activation(out=USB[:, m, c0:c0 + n], in_=banks[bk][:, 0:n], func=AF.Copy), reads=['bank%d' % bk], writes=['USB'])
                    else:
                        P.op('act', lambda e, m=m, bk=bk, c0=c0, n=n: e.activation(out=Z[:, m - 8, c0:c0 + n], in_=banks[bk][:, 0:n], func=AF.Copy), reads=['bank%d' % bk], writes=['Z'])
                    yield

        run(m2_gen(range(0, 8)))

        HCF = V(31056, [128, 64, 128], BF16)
        YPRE = V(35152, [128, 8, E], F32)
        VTW = V(44592, [128, 1 + NCH_W, 64], F32)
        VALLW = V(8192, [128, 64, NCH_W], F32)
        TMPQW = [V(12416 + i * 1056, [128, 16, NCH_W], F32) for i in range(2)]
        USBw = USB[:, :, 0:WIN].rearrange("p k (c s) -> p k c s", s=16)
        run(m2_gen(range(8, 16)), horner_gen(lambda k, s: USBw[:, k, :, s], 'USB', NCH_W, VALLW, TMPQW, vkp='VALW', two_step=True))
        P.barrier()
        P.op('dve', lambda e: e.tensor_copy(VTW[:, 0, :], HPRE[:, :]), reads=['HPRE'], writes=['vt'])
        P.op('pool', lambda e: e.tensor_copy(VTW[:, 1:1 + NCH_W, :].rearrange("p c j -> p j c"), VALLW[:, :, :]), reads=['VALW0', 'VALW1', 'VALW2', 'VALW3', 'vt'], writes=['vt'])
        Craw = V(14528, [128, 2, 64, 16], F32)
        P.dma('sp', lambda e: e.dma_start(out=V(14528, [128, 2, 1024], F32), in_=c_hp[:, :, :]), writes=['Craw'])
        for j in range(32):
            for c in range(2):
                mk = maskC if c == 0 else maskCn
                P.op('pool', lambda e, j=j, c=c, mk=mk: e.tensor_tensor(
                    out=CPAD[:, c * 32 + j, :].rearrange("p (a b) -> p a b", a=8),
                    in0=Craw[:, c, 8 * (j // 4):8 * (j // 4) + 8, :],
                    in1=mk[:, j % 4, :].unsqueeze(2).to_broadcast([128, 8, 16]), op=ALU.mult),
                    reads=['Craw', 'masks'], writes=['CPAD', 'BPAD1'])
        run(chunk_scan_gen(VTW, NCH_W, 'vt', e2='dve'))
        P.op('dve', lambda e: e.tensor_copy(SSMOUT[:, :, 4], VTW[:, NCH_W, :]), reads=['vt'], writes=['SSMOUT'])
        P.barrier()

        def y_block_gen(c0, n, HCAST=None, hck='HCAST'):
            HCAST = HCF if HCAST is None else HCAST
            for half in range(2):
                for k in range(4 * half, 4 * half + 4):
                    bk = 2 + k // 4
                    idx = 0
                    for j in range(4 * k, 4 * k + 4):
                        for c in range(2):
                            jcp = 16 * (j // 8) + 8 * c + (j % 8)
                            P.op('pe', lambda e, jcp=jcp, c=c, j=j, bk=bk, k=k, idx=idx: e.matmul(banks[bk][:, (k % 4) * 128:(k % 4) * 128 + n], CPAD[:, c * 32 + j, :], HCAST[:, jcp, 0:n], start=(idx == 0), stop=(idx == 7)),
                                 reads=['CPAD', hck], writes=['bank%d' % bk])
                            idx += 1
                    yield
                for k in range(4 * half, 4 * half + 4):
                    bk = 2 + k // 4
                    P.op('dve', lambda e, k=k, bk=bk: e.scalar_tensor_tensor(out=YPRE[:, k, c0:c0 + n], in0=USB[:, k, c0:c0 + n], scalar=dsk[:, k:k + 1], in1=banks[bk][:, (k % 4) * 128:(k % 4) * 128 + n], op0=ALU.mult, op1=ALU.add),
                         reads=['USB', 'vecs', 'bank%d' % bk], writes=['YPRE'])
                yield

        def y_block(c0, n, HCAST=None, hck='HCAST'):
            run(y_block_gen(c0, n, HCAST, hck))

        def scan_block_gen(hbw, nseq, hk, slot):
            a1 = A1[:, :].unsqueeze(2).to_broadcast([128, 64, nseq])
            a2v = A2[:, :].rearrange("p (q c j) -> p q c j", q=4, c=2)
            o = 256 * slot
            t2f = TMP2[:, o:o + 64 * nseq]
            t2 = t2f.rearrange("p (q c j i) -> p q c j i", q=4, c=2, j=8)
            t1 = TMP[:, o:o + 64 * nseq].rearrange("p (a i) -> p a i", i=nseq)
            ks = '_s%d' % slot
            for t in range(16):
                prev = hbw[:, t, :, :]
                cur = hbw[:, t + 1, :, :]
                prevv = prev.rearrange("p (q c j) i -> p q c j i", q=4, c=2)
                P.op('pool', lambda e, prevv=prevv: e.tensor_tensor(out=t2[:, :, 0, :, :], in0=a2v[:, :, 0, :].unsqueeze(3).to_broadcast([128, 4, 8, nseq]), in1=prevv[:, :, 1, :, :], op=ALU.mult), reads=[hk, 'A'], writes=['TMP2a' + ks])
                P.op('pool', lambda e, prevv=prevv: e.tensor_tensor(out=t2[:, :, 1, :, :], in0=a2v[:, :, 1, :].unsqueeze(3).to_broadcast([128, 4, 8, nseq]), in1=prevv[:, :, 0, :, :], op=ALU.mult), reads=[hk, 'A'], writes=['TMP2b' + ks])
                P.op('dve', lambda e, prev=prev: e.tensor_tensor(out=t1, in0=a1, in1=prev, op=ALU.mult), reads=[hk, 'A'], writes=['TMP' + ks])
                P.op('dve', lambda e, cur=cur: e.tensor_tensor(out=cur, in0=cur, in1=t1, op=ALU.add), reads=['TMP' + ks], writes=[hk])
                P.op('dve', lambda e, cur=cur: e.tensor_tensor(out=cur, in0=cur, in1=t2f.rearrange("p (a i) -> p a i", i=nseq), op=ALU.add), reads=['TMP2a' + ks, 'TMP2b' + ks], writes=[hk])
                yield

        def scan_multi_gen(blks):
            a2v = A2[:, :].rearrange("p (q c j) -> p q c j", q=4, c=2)
            info = []
            for (hbw, nseq, hk, slot) in blks:
                o = 256 * slot
                t2f = TMP2[:, o:o + 64 * nseq]
                info.append(dict(hbw=hbw, nseq=nseq, hk=hk, ks='_s%d' % slot,
                                 a1=A1[:, :].unsqueeze(2).to_broadcast([128, 64, nseq]),
                                 t2f=t2f, t2=t2f.rearrange("p (q c j i) -> p q c j i", q=4, c=2, j=8),
                                 t1=TMP[:, o:o + 64 * nseq].rearrange("p (a i) -> p a i", i=nseq)))
            for t in range(16):
                for d in info:
                    prevv = d['hbw'][:, t, :, :].rearrange("p (q c j) i -> p q c j i", q=4, c=2)
                    n_ = d['nseq']
                    P.op('pool', lambda e, d=d, prevv=prevv, n_=n_: e.tensor_tensor(out=d['t2'][:, :, 0, :, :], in0=a2v[:, :, 0, :].unsqueeze(3).to_broadcast([128, 4, 8, n_]), in1=prevv[:, :, 1, :, :], op=ALU.mult), reads=[d['hk'], 'A'], writes=['TMP2a' + d['ks']])
                    P.op('pool', lambda e, d=d, prevv=prevv, n_=n_: e.tensor_tensor(out=d['t2'][:, :, 1, :, :], in0=a2v[:, :, 1, :].unsqueeze(3).to_broadcast([128, 4, 8, n_]), in1=prevv[:, :, 0, :, :], op=ALU.mult), reads=[d['hk'], 'A'], writes=['TMP2b' + d['ks']])
                for d in info:
                    prev = d['hbw'][:, t, :, :]
                    P.op('dve', lambda e, d=d, prev=prev: e.tensor_tensor(out=d['t1'], in0=d['a1'], in1=prev, op=ALU.mult), reads=[d['hk'], 'A'], writes=['TMP' + d['ks']])
                for d in info:
                    cur = d['hbw'][:, t + 1, :, :]
                    P.op('dve', lambda e, d=d, cur=cur: e.tensor_tensor(out=cur, in0=cur, in1=d['t1'], op=ALU.add), reads=['TMP' + d['ks']], writes=[d['hk']])
                for d in info:
                    cur = d['hbw'][:, t + 1, :, :]
                    n_ = d['nseq']
                    P.op('dve', lambda e, d=d, cur=cur, n_=n_: e.tensor_tensor(out=cur, in0=cur, in1=d['t2f'].rearrange("p (a i) -> p a i", i=n_), op=ALU.add), reads=['TMP2a' + d['ks'], 'TMP2b' + d['ks']], writes=[d['hk']])
                yield

        BT = 32
        HBW4 = [V(8192 + i * 2176, [128, 17, 64, 2], F32) for i in range(4)]
        HC4 = [V(31056 + i * 1024, [128, 64, BT], BF16) for i in range(4)]
        nblk = WIN // BT
        pairs = [[b for b in (2 * k, 2 * k + 1) if b < nblk] for k in range((nblk + 1) // 2)]

        def prep_gen(pair):
            for b in pair:
                hbw = HBW4[b % 4]
                hk = 'hb%d' % (b % 4)
                c0 = b * BT
                P.op('pool', lambda e, b=b, hbw=hbw: e.tensor_copy(hbw[:, 0, :, :], VTW[:, 2 * b:2 * b + 2, :].rearrange("p c j -> p j c")), reads=['vt'], writes=[hk])
                for q in range(4):
                    bu_quarter(q, lambda k, c0=c0: USB[:, k, c0:c0 + BT], BT, 'USB')
                    P.op('act', lambda e, q=q, hbw=hbw: e.activation(
                        out=hbw[:, 1:17, 16 * q:16 * q + 16, :].rearrange("p t j c -> p j c t"),
                        in_=PSQ[:, :, 0:BT].rearrange("p j (c t) -> p j c t", t=16), func=AF.Copy),
                        reads=PSQK, writes=[hk])
                    yield

        def ypost_gen(pair):
            for b in pair:
                yield from y_block_gen(b * BT, BT, HC4[b % 4], 'HCAST%d' % (b % 4))

        run(prep_gen(pairs[0]))
        for k in range(len(pairs) + 1):
            gens = []
            if k < len(pairs):
                gens.append(scan_multi_gen([(HBW4[b % 4], 2, 'hb%d' % (b % 4), b % 2) for b in pairs[k]]))
            if k + 1 < len(pairs):
                gens.append(prep_gen(pairs[k + 1]))
            if k >= 1:
                gens.append(ypost_gen(pairs[k - 1]))
            run(*gens)
            if k < len(pairs):
                for b in pairs[k]:
                    hbw, hc = HBW4[b % 4], HC4[b % 4]
                    P.op('act', lambda e, hbw=hbw, hc=hc: e.activation(out=hc[:, :, :].rearrange("p j (c t) -> p j c t", t=16), in_=hbw[:, 1:17, :, :].rearrange("p t j c -> p j c t"), func=AF.Copy), reads=['hb%d' % (b % 4)], writes=['HCAST%d' % (b % 4)])
        P.barrier()
        ns = E - WIN
        HBS = V(8192, [128, 17, 64, 4], F32)
        P.op('pool', lambda e: e.tensor_copy(HBS[:, 0, :, :], STS[:, :, :]), reads=['STS'], writes=['hbS'])
        P.op('pool', lambda e: e.memset(HCF[:, :, 0:ns], 0.0), reads=[], writes=['HCAST'])
        for q in range(4):
            bu_quarter(q, lambda k: USB[:, k, WIN:E], ns, 'USB')
            for bb in range(4):
                P.op('act', lambda e, q=q, bb=bb: e.activation(
                    out=HBS[:, 1:17, 16 * q + 4 * bb:16 * q + 4 * bb + 4, :].rearrange("p t j c -> p j c t"),
                    in_=PSQ[:, 4 * bb:4 * bb + 4, 0:ns].rearrange("p j (c t) -> p j c t", c=NS)[:, :, :, 15:31], func=AF.Copy),
                    reads=['bank%d' % (4 + bb)], writes=['hbS'])
        run(scan_block_gen(HBS, NS, 'hbS', 0))
        P.op('act', lambda e: e.activation(out=HCF[:, :, 0:ns].rearrange("p j (c t) -> p j c t", c=NS)[:, :, :, 15:31], in_=HBS[:, 1:17, :, :].rearrange("p t j c -> p j c t"), func=AF.Copy), reads=['hbS'], writes=['HCAST'])
        P.op('dve', lambda e: e.tensor_copy(SSMOUT[:, :, 0:4], HBS[:, 16, :, :]), reads=['hbS'], writes=['SSMOUT'])
        y_block(WIN, ns)
        P.dma('sp', lambda e: e.dma_start(out=ssm_out[:, :, :], in_=SSMOUT[:]), reads=['SSMOUT'], is_out=True)
        P.barrier()

        YMIXB = V(0, [128, 16, E], BF16)
        YSB = V(9440, [128, 8, E], BF16)
        WGLU = V(14160, [128, 8, 1024], BF16)
        SG = [V(18256 + i * 512, [128, 512], F32) for i in range(2)]
        for k in range(8):
            P.dma('pool', lambda e, k=k: e.dma_start(out=WGLU[:, k, :], in_=w_glu[k * 128:(k + 1) * 128, :]), writes=['WGLU'])
        for k in range(8):
            P.op('act', lambda e, k=k: e.activation(out=YPRE[:, k, :], in_=YPRE[:, k, :], func=AF.Gelu_apprx_tanh), reads=['YPRE'], writes=['YPRE'])
            P.op('pool', lambda e, k=k: e.tensor_copy(YSB[:, k, :], YPRE[:, k, :]), reads=['YPRE'], writes=['YSB'])
        it = 0
        for m in range(8):
            for (c0, n) in BLKS:
                bk = 2 + it % 2
                sg = SG[it % 2]
                sk = 'SG%d' % (it % 2)
                it += 1
                for kk in range(8):
                    P.op('pe', lambda e, m=m, kk=kk, bk=bk, c0=c0, n=n: e.matmul(banks[bk][:, 0:n], WGLU[:, kk, m * 128:(m + 1) * 128], YSB[:, kk, c0:c0 + n], start=(kk == 0), stop=(kk == 7)),
                         reads=['WGLU', 'YSB'], writes=['bank%d' % bk])
                P.op('act', lambda e, bk=bk, sg=sg, n=n: e.activation(out=sg[:, 0:n], in_=banks[bk][:, 0:n], func=AF.Sigmoid), reads=['bank%d' % bk], writes=[sk])
                P.op('pool', lambda e, m=m, sg=sg, c0=c0, n=n: e.tensor_tensor(out=YMIXB[:, m, c0:c0 + n], in0=YPRE[:, m, c0:c0 + n], in1=sg[:, 0:n], op=ALU.mult), reads=['YPRE', sk], writes=['YMIXB'])
        P.barrier()

        PA = V(31056, [128, 2, E], F32)
        PB = V(33416, [128, 2, E], F32)
        MB = V(35776, [128, 8, E], BF16)
        RCN = V(40496, [128, 4, E], F32)
        WPOOL = V(45216, [128, 4, 2, 256], BF16)
        P.dma('sp', lambda e: e.dma_start(out=RCN, in_=cnt_d[:, :, :]), writes=['RCN'])
        P.dma('pool', lambda e: e.dma_start(out=WPOOL, in_=w_pool.rearrange("g (ki p) d -> p g ki d", p=128)), writes=['WPOOL'])
        Zs = Z[:, :, WIN:E].rearrange("p k (i c) -> p k i c", i=NS)
        for k in range(8):
            P.dma('sp', lambda e, k=k: e.dma_start(out=Zs[:, k, :, 0:15], in_=st_pool[:, k, :, :]), writes=['Z'])
        P.op('dve', lambda e: e.reciprocal(RCN, RCN), reads=['RCN'], writes=['RCN'])
        P.op('pool', lambda e: e.tensor_copy(POOLOUT[:, :, 4, :], Z[:, :, WIN - 15:WIN]), reads=['Z'], writes=['POOLOUT'])
        P.op('pool', lambda e: e.tensor_copy(POOLOUT[:, :, 0:4, :], Zs[:, :, :, 16:31]), reads=['Z'], writes=['POOLOUT'])
        P.dma('sp', lambda e: e.dma_start(out=pool_out[:, :, :, :], in_=POOLOUT[:]), reads=['POOLOUT'], is_out=True)
        PC = V(9440, [128, 2, E], F32)
        PD = V(11800, [128, 2, E], F32)
        for kk in [2, 0, 3, 1]:
            eng = 'dve' if kk >= 2 else 'pool'
            Zg = Z[:, 2 * kk:2 * kk + 2, :]
            cur, ck = Zg, 'Z'
            bufs = [(PC, 'PC'), (PD, 'PD')] if eng == 'dve' else [(PA, 'PA'), (PB, 'PB')]
            bi = 0
            for d in [1, 2, 4, 8][:kk + 1]:
                nxt, nk = bufs[bi % 2]
                bi += 1
                P.op(eng, lambda e, cur=cur, nxt=nxt, d=d: e.tensor_tensor(out=nxt[:, :, d:E], in0=cur[:, :, d:E], in1=cur[:, :, 0:E - d], op=ALU.add), reads=[ck], writes=[nk])
                P.op(eng, lambda e, cur=cur, nxt=nxt, d=d: e.tensor_copy(nxt[:, :, 0:d], cur[:, :, 0:d]), reads=[ck], writes=[nk])
                cur, ck = nxt, nk
            nxt, nk = bufs[bi % 2]
            P.op(eng, lambda e, cur=cur, nxt=nxt, kk=kk: e.tensor_tensor(out=nxt, in0=cur, in1=RCN[:, kk, :].unsqueeze(1).to_broadcast([128, 2, E]), op=ALU.mult), reads=[ck, 'RCN'], writes=[nk])
            P.op(eng, lambda e, nxt=nxt, kk=kk, Zg=Zg: e.tensor_tensor(out=MB[:, 2 * kk:2 * kk + 2, :], in0=nxt, in1=Zg, op=ALU.subtract), reads=[nk, 'Z'], writes=['MB%d' % kk])
            for mo in range(2):
                for (c0, n) in BLKS:
                    bk = 2 + it % 2
                    it += 1
                    for ki in range(2):
                        P.op('pe', lambda e, kk=kk, ki=ki, mo=mo, bk=bk, c0=c0, n=n: e.matmul(banks[bk][:, 0:n], WPOOL[:, kk, ki, mo * 128:(mo + 1) * 128], MB[:, 2 * kk + ki, c0:c0 + n], start=(ki == 0), stop=(ki == 1)),
                             reads=['WPOOL', 'MB%d' % kk], writes=['bank%d' % bk])
                    ch = 2 * kk + mo
                    P.op('act', lambda e, ch=ch, bk=bk, c0=c0, n=n: e.activation(out=YMIXB[:, 8 + ch, c0:c0 + n], in_=banks[bk][:, 0:n], func=AF.Identity, scale=psc[:, ch:ch + 1]), reads=['bank%d' % bk, 'vecs'], writes=['YMIXB'])
        P.barrier()

        WOUT = V(9440, [128, 16, 2048], BF16)
        X1 = V(28672, [128, 10, 2048], F32)
        HN2T = V(0, [128, 16, E], BF16)
        XSF = V(26456, [128, 2048], BF16)
        for k in range(16):
            for half in range(2):
                P.dma('pool', lambda e, k=k, half=half: e.dma_start(out=WOUT[:, k, half * 1024:(half + 1) * 1024], in_=w_out[k * 128:(k + 1) * 128, half * 1024:(half + 1) * 1024]), writes=['WOUT'])
        for tt, (r0, n) in enumerate(TILES):
            P.dma('sp', lambda e, tt=tt, r0=r0, n=n: e.dma_start(out=X1[0:n, tt, :], in_=xall[PRE + r0:PRE + r0 + n, :]), writes=['X1_%d' % tt])
        for tt, (r0, n) in enumerate(TILES):
            for nb in range(4):
                bk = 4 + it % 4
                it += 1
                for kk in range(16):
                    P.op('pe', lambda e, kk=kk, bk=bk, r0=r0, n=n, nb=nb: e.matmul(banks[bk][0:n, :], YMIXB[:, kk, r0:r0 + n], WOUT[:, kk, nb * 512:(nb + 1) * 512], start=(kk == 0), stop=(kk == 15)),
                         reads=['YMT%d' % tt, 'WOUT'], writes=['bank%d' % bk])
                P.op('dve', lambda e, tt=tt, bk=bk, n=n, nb=nb: e.tensor_tensor(out=X1[0:n, tt, nb * 512:(nb + 1) * 512], in0=X1[0:n, tt, nb * 512:(nb + 1) * 512], in1=banks[bk][0:n, :], op=ALU.add),
                     reads=['bank%d' % bk, 'X1_%d' % tt], writes=['X1_%d' % tt])
            if tt >= 1:
                pr0, pn = TILES[tt - 1]
                norm_transpose(X1[0:pn, tt - 1, :], pn, 'X1_%d' % (tt - 1), gffn, HN2T, 'HN2T', pr0, 0, xsbuf=XSF, extra_w=['YMT%d' % (tt - 1)])
        lr0, ln = TILES[-1]
        norm_transpose(X1[0:ln, len(TILES) - 1, :], ln, 'X1_%d' % (len(TILES) - 1), gffn, HN2T, 'HN2T', lr0, 0, xsbuf=XSF, extra_w=['YMT%d' % (len(TILES) - 1)])
        P.barrier()

        WUP = [[V(9440 + (s_ * 2 + gv) * 1024, [128, 16, 128], BF16) for gv in range(2)] for s_ in range(3)]
        WDN = [[V(15584 + (g_ * 2 + ci) * 1024, [128, 2048], BF16) for ci in range(2)] for g_ in range(2)]
        HG = [V(19680 + g_ * 1180, [128, 2, E], BF16) for g_ in range(2)]
        GBUF = [V(22040 + i * 1184, [128, E + 2], F32) for i in range(2)]
        GC = [V(24408 + i * 512, [128, 512], F32) for i in range(2)]
        for i in range(2):
            P.op('pool', lambda e, i=i: e.memset(GBUF[i][:, 0:2], 0.0), writes=['GBUF%d' % i])
        gcn = [0]

        def load_wup(j):
            s_ = j % 3
            for gv in range(2):
                P.dma('pool', lambda e, j=j, gv=gv, s_=s_: e.dma_start(out=WUP[s_][gv][:, :, :], in_=w_up[:, gv * DFF + j * 128:gv * DFF + (j + 1) * 128].rearrange("(k p) m -> p k m", p=128)), writes=['WUP%d_%d' % (s_, gv)])

        def load_wdn(jg):
            for ci in range(2):
                j = 2 * jg + ci
                for half in range(2):
                    P.dma('pool', lambda e, j=j, jg=jg, ci=ci, half=half: e.dma_start(out=WDN[jg % 2][ci][:, half * 1024:(half + 1) * 1024], in_=w_down[j * 128:(j + 1) * 128, half * 1024:(half + 1) * 1024]), writes=['WDN%d' % (jg % 2)])

        def up_unit(jg, ci, bi):
            hg = HG[jg % 2]
            hk = 'HG%d' % (jg % 2)
            j = 2 * jg + ci
            s_ = j % 3
            gb = GBUF[j % 2]
            gk = 'GBUF%d' % (j % 2)
            c0, n = BLKS[bi]
            pg = (j * 3 + bi) % 2
            pv = 2 + (j * 3 + bi) % 2
            for gv, bk in ((0, pg), (1, pv)):
                for kk in range(16):
                    P.op('pe', lambda e, gv=gv, bk=bk, kk=kk: e.matmul(banks[bk][:, 0:n], WUP[s_][gv][:, kk, :], HN2T[:, kk, c0:c0 + n], start=(kk == 0), stop=(kk == 15)),
                         reads=['WUP%d_%d' % (s_, gv), 'HN2T'], writes=['bank%d' % bk])
            P.op('act', lambda e: e.activation(out=gb[:, 2 + c0:2 + c0 + n], in_=banks[pg][:, 0:n], func=AF.Copy), reads=['bank%d' % pg], writes=[gk])
            if bi == 2:
                gs = gb[:, 2 + WIN:2 + E].rearrange("p (i c) -> p i c", i=NS)
                P.op('pool', lambda e: e.tensor_copy(CONVOUT[:, j, 4, :], gb[:, 2 + WIN - 2:2 + WIN]), reads=[gk], writes=['CONVOUT'])
                P.op('pool', lambda e: e.tensor_copy(CONVOUT[:, j, 0:4, :], gs[:, :, 29:31]), reads=[gk], writes=['CONVOUT'])
                P.op('pool', lambda e: e.tensor_copy(gs[:, :, 13:15], STC[:, j, :, :]), reads=['STC', gk], writes=[gk])
            gc = GC[gcn[0] % 2]
            gck = 'GC%d' % (gcn[0] % 2)
            gcn[0] += 1
            P.op('act', lambda e: e.activation(out=gc[:, 0:n], in_=gb[:, 2 + c0:2 + c0 + n], func=AF.Identity, scale=cw[:, 2, j:j + 1], bias=cb[:, j:j + 1]), reads=[gk, 'vecs'], writes=[gck])
            P.op('dve', lambda e: e.scalar_tensor_tensor(out=gc[:, 0:n], in0=gb[:, 1 + c0:1 + c0 + n], scalar=cw[:, 1, j:j + 1], in1=gc[:, 0:n], op0=ALU.mult, op1=ALU.add), reads=[gk, 'vecs', gck], writes=[gck])
            P.op('dve', lambda e: e.scalar_tensor_tensor(out=gc[:, 0:n], in0=gb[:, c0:c0 + n], scalar=cw[:, 0, j:j + 1], in1=gc[:, 0:n], op0=ALU.mult, op1=ALU.add), reads=[gk, 'vecs', gck], writes=[gck])
            P.op('act', lambda e: e.activation(out=gc[:, 0:n], in_=gc[:, 0:n], func=AF.Gelu_apprx_tanh), reads=[gck], writes=[gck])
            P.op('dve', lambda e: e.tensor_tensor(out=hg[:, ci, c0:c0 + n], in0=gc[:, 0:n], in1=banks[pv][:, 0:n], op=ALU.mult), reads=[gck, 'bank%d' % pv], writes=[hk])

        dcnt = [0]

        def down_unit(jg, tt, nb):
            hg = HG[jg % 2]
            hk = 'HG%d' % (jg % 2)
            r0, n = TILES[tt]
            bk = 4 + dcnt[0] % 4
            dcnt[0] += 1
            for ci in range(2):
                P.op('pe', lambda e, ci=ci: e.matmul(banks[bk][0:n, :], hg[:, ci, r0:r0 + n], WDN[jg % 2][ci][:, nb * 512:(nb + 1) * 512], start=(ci == 0), stop=(ci == 1)),
                     reads=[hk, 'WDN%d' % (jg % 2)], writes=['bank%d' % bk])
            P.op('dve', lambda e: e.tensor_tensor(out=X1[0:n, tt, nb * 512:(nb + 1) * 512], in0=X1[0:n, tt, nb * 512:(nb + 1) * 512], in1=banks[bk][0:n, :], op=ALU.add),
                 reads=['bank%d' % bk, 'X1_%d' % tt], writes=['X1_%d' % tt])

        NG = NJ // 2
        load_wup(0)
        pending = []
        for jg in range(NG):
            load_wdn(jg)
            ups = [(ci, bi) for ci in range(2) for bi in range(3)]
            per = (len(pending) + len(ups) - 1) // len(ups)
            for ui, (ci, bi) in enumerate(ups):
                if bi == 0:
                    jn = 2 * jg + ci + 1
                    if jn < NJ:
                        load_wup(jn)
                up_unit(jg, ci, bi)
                for _ in range(per):
                    if pending:
                        down_unit(*pending.pop(0))
            while pending:
                down_unit(*pending.pop(0))
            pending = [(jg, tt, nb) for tt in range(len(TILES)) for nb in range(4)]
        P.dma('sp', lambda e: e.dma_start(out=conv_out[:, :, :, :], in_=CONVOUT[:]), reads=['CONVOUT'], is_out=True)

        GFIN = V(0, [128, 2048], F32)
        YOUT = [V(2048 + i * 2048, [128, 2048], F32) for i in range(2)]
        P.op('dve', lambda e: e.memset(YOUT[0][:, 0:1], 0.0), writes=['YOUT0', 'YOUT1', 'GFIN', 'HN2T'])
        P.dma('sp', lambda e: e.dma_start(out=GFIN, in_=gfin_d[:, :]), writes=['GFIN'])

        def final_tile(tt):
            r0, n = TILES[tt]
            yo = YOUT[tt % 2]
            yk = 'YOUT%d' % (tt % 2)
            s_, k = rms_rstd(X1[0:n, tt, :], n, 'X1_%d' % tt, junk=yo, junkkey=yk)
            P.op('dve', lambda e: e.scalar_tensor_tensor(out=yo[0:n, :], in0=X1[0:n, tt, :], scalar=s_, in1=GFIN[0:n, :], op0=ALU.mult, op1=ALU.mult), reads=['X1_%d' % tt, k, 'GFIN'], writes=[yk])
            P.dma('sp', lambda e: e.dma_start(out=y_out[r0:r0 + n, :], in_=yo[0:n, :]), reads=[yk], is_out=True)

        while pending:
            u = pending.pop(0)
            down_unit(*u)
            if u[2] == 3:
                final_tile(u[1])
        P.finish()
    return nc


_NC = None


def kernel(x_prompt, x_sample, state_ssm_re, state_ssm_im, state_pool, state_ffn_conv,
           meta_tokens, g_mix, w_in, lam_re, lam_im, log_dt, b_re, b_im, c_re, c_im,
           d_skip, w_glu, w_pool, pool_scale, w_out, g_ffn, w_up, conv_w, conv_b,
           w_down, g_final):
    global _NC
    f = lambda a: np.ascontiguousarray(np.asarray(a, dtype=np.float32))
    x_prompt, x_sample = f(x_prompt), f(x_sample)
    meta = f(meta_tokens)
    B = x_prompt.shape[0]
    lam_pg = np.stack([f(lam_re)[0].T, f(lam_im)[0].T, np.broadcast_to(f(log_dt)[0][None, :], (64, 64))], axis=1)
    hp = lambda a: a.reshape(32, 2, 64).transpose(1, 2, 0).reshape(128, 32)
    ldt_hp = np.broadcast_to(f(log_dt)[0].reshape(32, 2).T[:, None, :], (2, 64, 32)).reshape(128, 32)
    lam_hp = np.stack([hp(f(lam_re)[0]), hp(f(lam_im)[0]), ldt_hp], axis=1)
    gm = lambda a: np.broadcast_to(a.reshape(8, 8, 1, 64).transpose(1, 2, 0, 3), (8, 16, 8, 64)).reshape(128, 512)
    ldt_gm = np.broadcast_to(f(log_dt)[0].reshape(8, 8, 1, 1).transpose(1, 2, 0, 3), (8, 16, 8, 64)).reshape(128, 512)
    lam_gm = np.stack([gm(f(lam_re)[0]), gm(f(lam_im)[0]), ldt_gm], axis=1)
    b_pg = np.stack([f(b_re)[0].transpose(1, 0, 2).reshape(64, 1024), f(b_im)[0].transpose(1, 0, 2).reshape(64, 1024)], axis=1)
    cpg = np.stack([f(c_re)[0].transpose(2, 0, 1).reshape(64, 1024), f(c_im)[0].transpose(2, 0, 1).reshape(64, 1024)], axis=1)
    c_hp = np.concatenate([cpg, cpg], axis=0)
    vecs = np.zeros((128, 224), np.float32)
    vecs[:, 0:16] = f(g_mix)[0].reshape(16, 128).T
    vecs[:, 16:32] = f(g_ffn)[0].reshape(16, 128).T
    vecs[:, 32:40] = f(d_skip)[0].reshape(8, 128).T
    vecs[:, 40:48] = f(pool_scale)[0].reshape(8, 128).T
    vecs[:, 48:180] = f(conv_w)[0].reshape(3, 44, 128).transpose(2, 0, 1).reshape(128, 132)
    vecs[:, 180:224] = f(conv_b)[0].reshape(44, 128).T
    gfin = np.broadcast_to(f(g_final)[None, :], (128, D))
    r = np.arange(128)
    maskB = np.zeros((128, 4, 2), np.float32)
    maskC = np.zeros((128, 4, 8), np.float32)
    for jj in range(4):
        for h in range(2):
            maskB[:, jj, h] = (r // 16 == 2 * jj + h)
        for gg in range(8):
            maskC[:, jj, gg] = (gg == 2 * jj + r // 64)
    masks = np.concatenate([maskB.reshape(128, 8), maskC.reshape(128, 32), -maskC.reshape(128, 32)], axis=1)
    shared = dict(lam_pg=f(lam_pg), lam_hp=f(lam_hp), lam_gm=f(lam_gm), b_pg=f(b_pg), c_hp=f(c_hp), vecs=vecs, gfin=f(gfin),
                  ident=np.eye(128, dtype=np.float32), masks=f(masks),
                  w_in=f(w_in)[0], w_glu=f(w_glu)[0], w_pool=f(w_pool)[0], w_out=f(w_out)[0],
                  w_up=f(w_up)[0], w_down=f(w_down)[0])
    sre, sim = f(state_ssm_re)[0], f(state_ssm_im)[0]
    spool, sconv = f(state_pool)[0], f(state_ffn_conv)[0]
    wins = np.array([2, 4, 8, 16], np.float32)
    in_maps = []
    for c in range(8):
        p, s = c // 4, c % 4
        seq = np.concatenate([meta, x_prompt[p]], axis=0)
        n = SEG * (s + 1)
        xall = np.zeros((PRE + E, D), np.float32)
        xall[PRE + WIN - n:PRE + WIN] = seq[:n]
        for i in range(NS):
            xall[PRE + WIN + 31 * i + 15:PRE + WIN + 31 * i + 31] = x_sample[NS * c + i]
        sl = slice(NS * c, NS * c + NS)
        t4 = lambda a: a.reshape(NS, 32, 2, 64).transpose(2, 3, 1, 0).reshape(128, 32, NS)
        st_ssm = np.stack([t4(sre[sl]).reshape(128, 4, 8, NS), t4(sim[sl]).reshape(128, 4, 8, NS)], axis=2).reshape(128, 64, NS)
        st_pool = spool[sl].reshape(NS, 15, 8, 128).transpose(3, 2, 0, 1)
        st_conv = sconv[sl].reshape(NS, 2, NJ, 128).transpose(3, 2, 0, 1)
        pos = SEG * s - OV + np.arange(WIN)
        cnt = np.empty((4, E), np.float32)
        for k in range(4):
            cnt[k, :WIN] = np.clip(np.minimum(pos + 1, wins[k]), 1, wins[k])
            cnt[k, WIN:] = wins[k]
        d = dict(shared)
        d.update(xall=xall, st_ssm=f(st_ssm), st_pool=f(st_pool), st_conv=f(st_conv),
                 cnt=f(np.broadcast_to(cnt[None], (128, 4, E))))
        in_maps.append(d)
    if _NC is None:
        _NC = build_nc()
    res = run_bass_kernel_spmd(_NC, in_maps, core_ids=list(range(8)))
    R = res.results
    S = x_prompt.shape[1]
    y_prompt = np.zeros((B, S, D), np.float32)
    y_sample = np.zeros((x_sample.shape[0], 16, D), np.float32)
    ssm_re_p = np.zeros((1, B, 64, 64), np.float32)
    ssm_im_p = np.zeros((1, B, 64, 64), np.float32)
    pool_p = np.zeros((1, B, 15, 1024), np.float32)
    conv_p = np.zeros((1, B, 2, DFF), np.float32)
    ssm_re_s = np.zeros((1, 32, 64, 64), np.float32)
    ssm_im_s = np.zeros((1, 32, 64, 64), np.float32)
    pool_s = np.zeros((1, 32, 15, 1024), np.float32)
    conv_s = np.zeros((1, 32, 2, DFF), np.float32)

    def ssm_unpack(a):
        a4 = a.reshape(128, 4, 2, 8)
        un = lambda x: x.reshape(2, 64, 32).transpose(2, 0, 1).reshape(64, 64)
        return un(a4[:, :, 0, :].reshape(128, 32)), un(a4[:, :, 1, :].reshape(128, 32))

    for c in range(8):
        p, s = c // 4, c % 4
        y = np.asarray(R[c]["y_out"], dtype=np.float32)
        so = np.asarray(R[c]["ssm_out"], dtype=np.float32)
        po = np.asarray(R[c]["pool_out"], dtype=np.float32)
        co = np.asarray(R[c]["conv_out"], dtype=np.float32)
        pos0 = SEG * s
        lo = max(pos0, 16)
        y_prompt[p, lo - 16:pos0 + SEG - 16] = y[OV + (lo - pos0):WIN]
        for i in range(NS):
            q = NS * c + i
            y_sample[q] = y[WIN + 31 * i + 15:WIN + 31 * i + 31]
            ssm_re_s[0, q], ssm_im_s[0, q] = ssm_unpack(so[:, :, i])
            pool_s[0, q] = po[:, :, i, :].transpose(2, 1, 0).reshape(15, 1024)
            conv_s[0, q] = co[:, :, i, :].transpose(2, 1, 0).reshape(2, DFF)
        if s == 3:
            ssm_re_p[0, p], ssm_im_p[0, p] = ssm_unpack(so[:, :, 4])
            pool_p[0, p] = po[:, :, 4, :].transpose(2, 1, 0).reshape(15, 1024)
            conv_p[0, p] = co[:, :, 4, :].transpose(2, 1, 0).reshape(2, DFF)
    return (y_prompt, y_sample, ssm_re_p, ssm_im_p, pool_p, conv_p,
            ssm_re_s, ssm_im_s, pool_s, conv_s)
```
